# Optimizing a Trainium2 kernel written in Bass

```python
import jax, jax.numpy as jnp
from jax import lax
import numpy as np

D_MODEL = 2048
BATCH = 1
SEQ = 8192
DEPTH = 1
DEC_BATCH = 32
DEC_SEQ = 16
PAST_LEN = 1024

CHUNK = 64
N_META = 16
EPS = 1e-6

GLA_HEADS = 4
GLA_DK = D_MODEL // 2 // GLA_HEADS
GLA_DV = D_MODEL // GLA_HEADS
GLA_RANK = 16
GLA_GATE_NORM = 16.0
GLA_BLOCK = 16
GLA_LOG_ALPHA_MIN = -5.0

MLA_HEADS = 16
MLA_Q_RANK = 512
MLA_KV_RANK = 512
MLA_NOPE = 128
MLA_ROPE = 64
MLA_V = 128
ROPE_THETA = 10000.0
Q_BLOCK = 128

D_FF = 5632
CONV_W = 3

GLA_QK = GLA_HEADS * GLA_DK
GLA_VW = GLA_HEADS * GLA_DV
MLA_QW = MLA_HEADS * (MLA_NOPE + MLA_ROPE)
IN_SPLITS = (GLA_QK, GLA_QK, GLA_VW, GLA_VW, GLA_RANK, MLA_Q_RANK, MLA_KV_RANK, MLA_ROPE, D_MODEL, D_MODEL)
IN_COLS = sum(IN_SPLITS)

kernel_name = "hybrid_gla_mla_convffn_stream_step"


def rmsnorm(x, g):
    x32 = x.astype(jnp.float32)
    y = x32 * lax.rsqrt(jnp.mean(x32 * x32, axis=-1, keepdims=True) + EPS)
    return (y * g.astype(jnp.float32)).astype(x.dtype)


def split_cols(h):
    out, start = [], 0
    for w in IN_SPLITS:
        out.append(h[..., start:start + w])
        start += w
    return out


def rope_angles(pos):
    inv = ROPE_THETA ** (-jnp.arange(0, MLA_ROPE, 2, dtype=jnp.float32) / MLA_ROPE)
    ang = pos.astype(jnp.float32)[:, None] * inv[None, :]
    return jnp.cos(ang), jnp.sin(ang)


def apply_rope(x, cos, sin):
    x32 = x.astype(jnp.float32)
    x1, x2 = x32[..., :MLA_ROPE // 2], x32[..., MLA_ROPE // 2:]
    return jnp.concatenate([x1 * cos - x2 * sin, x2 * cos + x1 * sin], -1).astype(x.dtype)


def gla_recurrence(q, k, v, log_a, s0):
    B, T, H, _ = q.shape
    pad = (-T) % GLA_BLOCK
    nb = (T + pad) // GLA_BLOCK

    def blk(a):
        a = jnp.pad(a.astype(jnp.float32), ((0, 0), (0, pad), (0, 0), (0, 0)))
        return a.reshape(B, nb, GLA_BLOCK, H, a.shape[-1])

    q, k, v, log_a = blk(q), blk(k), blk(v), blk(log_a)
    b = jnp.cumsum(log_a, axis=2)
    b_last = b[:, :, -1:]
    q_dec = q * jnp.exp(b)
    k_inv = k * jnp.exp(-b)
    k_end = k * jnp.exp(b_last - b)
    causal = jnp.tril(jnp.ones((GLA_BLOCK, GLA_BLOCK), dtype=bool))
    att = jnp.where(causal, jnp.einsum("bnchd,bnshd->bnhcs", q_dec, k_inv), 0.0)
    o_intra = jnp.einsum("bnhcs,bnshe->bnche", att, v)
    decay = jnp.exp(b_last[:, :, 0])

    def step(S, inp):
        q_n, k_n, v_n, d_n = inp
        o_n = jnp.einsum("bchd,bhde->bche", q_n, S)
        S = d_n[..., None] * S + jnp.einsum("bchd,bche->bhde", k_n, v_n)
        return S, o_n

    xs = tuple(jnp.moveaxis(a, 1, 0) for a in (q_dec, k_end, v, decay))
    S, o_inter = lax.scan(step, s0.astype(jnp.float32), xs)
    o = o_intra + jnp.moveaxis(o_inter, 0, 1)
    return o.reshape(B, nb * GLA_BLOCK, H, v.shape[-1])[:, :T], S


def attend(q_nope, q_pe, k_nope, k_pe, v, q_chunk, k_chunk):
    scale = (MLA_NOPE + MLA_ROPE) ** -0.5
    s = (jnp.einsum("bqhd,bkhd->bhqk", q_nope, k_nope).astype(jnp.float32)
         + jnp.einsum("bqhr,bkr->bhqk", q_pe, k_pe).astype(jnp.float32)) * scale
    visible = q_chunk[:, None] >= k_chunk[None, :]
    p = jax.nn.softmax(jnp.where(visible, s, -1e30), axis=-1)
    return jnp.einsum("bhqk,bkhe->bqhe", p.astype(v.dtype), v)


def block_causal_attention(q_nope, q_pe, k_nope, k_pe, v, q_chunk, k_chunk):
    B, Tq = q_nope.shape[:2]
    if Tq <= Q_BLOCK:
        return attend(q_nope, q_pe, k_nope, k_pe, v, q_chunk, k_chunk)
    pad = (-Tq) % Q_BLOCK
    nq = (Tq + pad) // Q_BLOCK

    def blocks(a):
        a = jnp.pad(a, [(0, 0), (0, pad)] + [(0, 0)] * (a.ndim - 2))
        return jnp.moveaxis(a.reshape(B, nq, Q_BLOCK, *a.shape[2:]), 1, 0)

    qc = jnp.pad(q_chunk, (0, pad), mode="edge").reshape(nq, Q_BLOCK)
    out = lax.map(lambda a: attend(a[0], a[1], k_nope, k_pe, v, a[2], k_chunk),
                  (blocks(q_nope), blocks(q_pe), qc))
    out = jnp.moveaxis(out, 0, 1).reshape(B, nq * Q_BLOCK, MLA_HEADS, v.shape[-1])
    return out[:, :Tq]


def trunk_layer(x, pos, q_chunk, past_latent, past_krope, past_chunk, gla_state, conv_hist,
                g_mix, w_in, w_a2, b_a, g_gla_out, g_q, w_uq, g_kv, w_uk, w_uv, w_o,
                g_ffn, w_up, conv_w, conv_b, w_down):
    B, T, _ = x.shape
    h = rmsnorm(x, g_mix)
    q_g, k_g, v_g, r_g, a_g, c_q, c_kv, k_pe, gate_a, gate_b = split_cols(h @ w_in)

    q_g = q_g.reshape(B, T, GLA_HEADS, GLA_DK) * (GLA_DK ** -0.5)
    k_g = k_g.reshape(B, T, GLA_HEADS, GLA_DK)
    v_g = v_g.reshape(B, T, GLA_HEADS, GLA_DV)
    log_a = jax.nn.log_sigmoid((a_g @ w_a2 + b_a).astype(jnp.float32)) / GLA_GATE_NORM
    log_a = jnp.maximum(log_a, GLA_LOG_ALPHA_MIN).reshape(B, T, GLA_HEADS, GLA_DK)
    o_g, new_state = gla_recurrence(q_g, k_g, v_g, log_a, gla_state)
    o_g = rmsnorm(o_g.astype(x.dtype), g_gla_out).reshape(B, T, GLA_VW)
    branch_a = o_g * jax.nn.silu(r_g)

    q = (rmsnorm(c_q, g_q) @ w_uq).reshape(B, T, MLA_HEADS, MLA_NOPE + MLA_ROPE)
    q_nope, q_pe = q[..., :MLA_NOPE], q[..., MLA_NOPE:]
    latent = rmsnorm(c_kv, g_kv)
    cos, sin = rope_angles(pos)
    k_pe = apply_rope(k_pe, cos, sin)
    q_pe = apply_rope(q_pe, cos[:, None], sin[:, None])
    all_latent = jnp.concatenate([past_latent.astype(latent.dtype), latent], 1)
    all_kpe = jnp.concatenate([past_krope.astype(k_pe.dtype), k_pe], 1)
    Tk = all_latent.shape[1]
    k_nope = (all_latent @ w_uk).reshape(B, Tk, MLA_HEADS, MLA_NOPE)
    v = (all_latent @ w_uv).reshape(B, Tk, MLA_HEADS, MLA_V)
    k_chunk = jnp.concatenate([past_chunk, q_chunk])
    o_m = block_causal_attention(q_nope, q_pe, k_nope, all_kpe, v, q_chunk, k_chunk)
    branch_b = o_m.reshape(B, T, MLA_HEADS * MLA_V)

    merged = jax.nn.sigmoid(gate_a) * branch_a + jax.nn.sigmoid(gate_b) * branch_b
    x = x + merged @ w_o

    u = rmsnorm(x, g_ffn) @ w_up
    u_ext = jnp.concatenate([conv_hist.astype(u.dtype), u], 1)
    c = conv_b + conv_w[0] * u_ext[:, 0:T]
    for j in range(1, CONV_W):
        c = c + conv_w[j] * u_ext[:, j:j + T]
    x = x + (jax.nn.silu(c[..., :D_FF]) * c[..., D_FF:]) @ w_down
    return x, latent, k_pe, new_state.astype(x.dtype), u_ext[:, -(CONV_W - 1):]


def setup_inputs(seed: int = 0) -> dict:
    key = jax.random.key(seed)
    ks = jax.random.split(key, 32)
    f32 = jnp.float32

    def nrm(k, shape, scale):
        return scale * jax.random.normal(k, shape, f32)

    def gain(k, shape):
        return 1.0 + 0.02 * jax.random.normal(k, shape, f32)

    return {
        "x_prompt": nrm(ks[0], (BATCH, SEQ, D_MODEL), 1.0),
        "x_sample": nrm(ks[1], (DEC_BATCH, DEC_SEQ, D_MODEL), 1.0),
        "cache_mla_latent": nrm(ks[2], (DEPTH, DEC_BATCH, PAST_LEN, MLA_KV_RANK), 1.0),
        "cache_mla_krope": nrm(ks[3], (DEPTH, DEC_BATCH, PAST_LEN, MLA_ROPE), 1.0),
        "state_gla": nrm(ks[4], (DEPTH, DEC_BATCH, GLA_HEADS, GLA_DK, GLA_DV), 1.0),
        "cache_ffn_conv": nrm(ks[5], (DEPTH, DEC_BATCH, CONV_W - 1, 2 * D_FF), 1.0),
        "meta_tokens": nrm(ks[6], (N_META, D_MODEL), 1.0),
        "g_mix": gain(ks[7], (DEPTH, D_MODEL)),
        "w_in": nrm(ks[8], (DEPTH, D_MODEL, IN_COLS), D_MODEL ** -0.5),
        "w_a2": nrm(ks[9], (DEPTH, GLA_RANK, GLA_QK), GLA_RANK ** -0.5),
        "b_a": nrm(ks[10], (DEPTH, GLA_QK), 0.1),
        "g_gla_out": gain(ks[11], (DEPTH, GLA_DV)),
        "g_q": gain(ks[12], (DEPTH, MLA_Q_RANK)),
        "w_uq": nrm(ks[13], (DEPTH, MLA_Q_RANK, MLA_QW), MLA_Q_RANK ** -0.5),
        "g_kv": gain(ks[14], (DEPTH, MLA_KV_RANK)),
        "w_uk": nrm(ks[15], (DEPTH, MLA_KV_RANK, MLA_HEADS * MLA_NOPE), MLA_KV_RANK ** -0.5),
        "w_uv": nrm(ks[16], (DEPTH, MLA_KV_RANK, MLA_HEADS * MLA_V), MLA_KV_RANK ** -0.5),
        "w_o": nrm(ks[17], (DEPTH, D_MODEL, D_MODEL), D_MODEL ** -0.5),
        "g_ffn": gain(ks[18], (DEPTH, D_MODEL)),
        "w_up": nrm(ks[19], (DEPTH, D_MODEL, 2 * D_FF), D_MODEL ** -0.5),
        "conv_w": nrm(ks[20], (DEPTH, CONV_W, 2 * D_FF), CONV_W ** -0.5),
        "conv_b": nrm(ks[21], (DEPTH, 2 * D_FF), 0.02),
        "w_down": nrm(ks[22], (DEPTH, D_FF, D_MODEL), D_FF ** -0.5),
        "final_norm": gain(ks[23], (D_MODEL,)),
    }


def reference(x_prompt, x_sample, cache_mla_latent, cache_mla_krope, state_gla, cache_ffn_conv,
              meta_tokens, g_mix, w_in, w_a2, b_a, g_gla_out, g_q, w_uq, g_kv, w_uk, w_uv, w_o,
              g_ffn, w_up, conv_w, conv_b, w_down, final_norm):
    B, T_p, _ = x_prompt.shape
    T_s = x_sample.shape[1]
    P = cache_mla_latent.shape[2]
    dt = x_prompt.dtype

    xp = jnp.concatenate([jnp.broadcast_to(meta_tokens.astype(dt)[None], (B, N_META, D_MODEL)), x_prompt], 1)
    idx = jnp.arange(T_p + N_META, dtype=jnp.int32)
    chunk_p = jnp.where(idx < N_META, -1, (idx - N_META) // CHUNK).astype(jnp.int32)
    no_chunk = jnp.zeros((0,), jnp.int32)
    pos_s = P + jnp.arange(T_s, dtype=jnp.int32)
    chunk_s = pos_s // CHUNK
    past_chunk_s = jnp.arange(P, dtype=jnp.int32) // CHUNK
    xs = x_sample

    lat_p, kpe_p, gla_p, conv_p = [], [], [], []
    lat_s, kpe_s, gla_s, conv_s = [], [], [], []
    for l in range(DEPTH):
        wl = (g_mix[l], w_in[l], w_a2[l], b_a[l], g_gla_out[l], g_q[l], w_uq[l], g_kv[l],
              w_uk[l], w_uv[l], w_o[l], g_ffn[l], w_up[l], conv_w[l], conv_b[l], w_down[l])
        xp, lat, kpe, st, cv = trunk_layer(
            xp, idx, chunk_p,
            jnp.zeros((B, 0, MLA_KV_RANK), dt), jnp.zeros((B, 0, MLA_ROPE), dt), no_chunk,
            jnp.zeros((B, GLA_HEADS, GLA_DK, GLA_DV), jnp.float32),
            jnp.zeros((B, CONV_W - 1, 2 * D_FF), dt), *wl)
        lat_p.append(lat); kpe_p.append(kpe); gla_p.append(st); conv_p.append(cv)
        xs, lat, kpe, st, cv = trunk_layer(
            xs, pos_s, chunk_s, cache_mla_latent[l], cache_mla_krope[l], past_chunk_s,
            state_gla[l], cache_ffn_conv[l], *wl)
        lat_s.append(lat); kpe_s.append(kpe); gla_s.append(st); conv_s.append(cv)

    y_prompt = rmsnorm(xp, final_norm)[:, N_META:]
    y_sample = rmsnorm(xs, final_norm)
    return (y_prompt, y_sample,
            jnp.stack(lat_p), jnp.stack(kpe_p), jnp.stack(gla_p), jnp.stack(conv_p),
            jnp.stack(lat_s), jnp.stack(kpe_s), jnp.stack(gla_s), jnp.stack(conv_s))
```

```python
from contextlib import ExitStack
import numpy as np
import ml_dtypes
import concourse.bass as bass
import concourse.mybir as mybir
from concourse.bass_utils import run_bass_kernel_spmd

F32 = mybir.dt.float32
BF16 = mybir.dt.bfloat16
AF = mybir.ActivationFunctionType
ALU = mybir.AluOpType

NCORES = 8
D = 2048
KC = 16
SEQ = 8192
NMETA = 16
EXT = SEQ + NMETA
NPRE = 7168
NPT = NPRE // 128
WINP = 1040
NSS = 4
TS = 16
NT = WINP + NSS * TS
PAST = 1024
EPS = 1e-6
GH, GDK, GDV = 4, 256, 512
MH, NOPE, ROPE, MV = 16, 128, 64, 128
QR, KVR = 512, 512
DFF = 5632
FC = 2 * DFF // 128
GC = DFF // 128
O_Q, O_K, O_V, O_R, O_A = 0, 1024, 2048, 4096, 6144
O_CQ, O_CKV, O_KPE, O_GA, O_GB = 6160, 6672, 7184, 7248, 9296
INC = 11344
SCALE = (NOPE + ROPE) ** -0.5
NEG = -30000.0
import os
NPT_RUN = NPT
NPRE_BLK = int(os.environ.get('K_NPRE_BLK', NPRE // 512))
NSS_RUN = int(os.environ.get('K_NSS_RUN', NSS))
K_P2 = int(os.environ.get('K_P2', 2))
GH_RUN = int(os.environ.get('K_GH', GH))
K_DBG = int(os.environ.get('K_DBG', 0))
MH_RUN = int(os.environ.get('K_MH', MH))
NPT_RUN = int(os.environ.get('K_NPT', NPT))
K_3B = int(os.environ.get('K_3B', 127))


class Tok:
    __slots__ = ("sem", "val", "eng", "sid")

    def __init__(self, sem, val, eng, sid):
        self.sem, self.val, self.eng, self.sid = sem, val, eng, sid


class Buf:
    def __init__(self, name):
        self.name = name
        self.w = None
        self.r = []
        self.dsem = None
        self.dcnt = 0


class K:
    def __init__(self, nc):
        self.nc = nc
        self.engs = {"pe": nc.tensor, "act": nc.scalar, "dve": nc.vector, "pool": nc.gpsimd, "sp": nc.sync}
        self.sem = {}
        self.cnt = {}
        self.waited = {e: {} for e in self.engs}
        self.nsem = 0
        for e in ("pe", "act", "dve", "pool"):
            self.sem[e] = nc.alloc_semaphore("prog_" + e)
            self.cnt[e] = 0
        self.pe_pending = []
        self.bufs = {}
        self.dsems = {}

    def buf(self, name):
        b = self.bufs.get(name)
        if b is None:
            b = Buf(name)
            self.bufs[name] = b
        return b

    def _wait(self, e, tok):
        if tok is None:
            return
        if tok.eng == "pe" and e == "pe":
            return
        w = self.waited[e]
        if w.get(tok.sid, 0) >= tok.val:
            return
        self.engs[e].wait_ge(tok.sem, tok.val)
        w[tok.sid] = tok.val

    def _deps(self, e, reads, writes):
        need = {}

        def want(tok):
            if tok is None:
                return
            cur = need.get(tok.sid)
            if cur is None or tok.val > cur.val:
                need[tok.sid] = tok

        for b in reads:
            want(b.w)
        for b in writes:
            if b in self.pe_pending and e != "pe":
                raise RuntimeError("write to buffer with unmarked PE reads: " + b.name)
            want(b.w)
            for r in b.r:
                if r.eng == e and e not in ("sp",):
                    continue
                want(r)
        for tok in need.values():
            self._wait(e, tok)

    def _commit(self, tok, reads, writes):
        for b in reads:
            b.r.append(tok)
        for b in writes:
            b.w = tok
            b.r = []

    def op(self, e, fn, reads=(), writes=(), mark=True):
        reads = [self.buf(b) if isinstance(b, str) else b for b in reads]
        writes = [self.buf(b) if isinstance(b, str) else b for b in writes]
        self._deps(e, reads, writes)
        ins = fn(self.engs[e])
        if not mark:
            assert e == "pe"
            for b in reads:
                if b not in self.pe_pending:
                    self.pe_pending.append(b)
            return None
        self.cnt[e] += 1
        ins.then_inc(self.sem[e], 1)
        tok = Tok(self.sem[e], self.cnt[e], e, "prog_" + e)
        if e == "pe" and self.pe_pending:
            for b in self.pe_pending:
                b.r.append(tok)
            self.pe_pending = []
        self._commit(tok, reads, writes)
        return tok

    def dma(self, q, out, in_, reads=(), writes=(), sb=None):
        reads = [self.buf(b) for b in reads if b not in DRAM_NAMES]
        writes = [self.buf(b) for b in writes if b not in DRAM_NAMES]
        sb = self.buf(sb) if isinstance(sb, str) else sb
        ds = self.dsems.get(sb.name)
        if ds is None:
            ds = self.dsems[sb.name] = [self.nc.alloc_semaphore("d_" + sb.name), 0]
            self.nsem += 1
        self._deps(q, reads, writes)
        ins = self.engs[q].dma_start(out=out, in_=in_)
        ds[1] += 16
        ins.then_inc(ds[0], 16)
        tok = Tok(ds[0], ds[1], "dma", "d_" + sb.name)
        self._commit(tok, reads, writes)
        return tok

    def barrier(self):
        assert not self.pe_pending
        toks = [Tok(self.sem[e], self.cnt[e], e, "prog_" + e) for e in self.sem if self.cnt[e] > 0]
        toks += [Tok(d[0], d[1], "dma", "d_" + n) for n, d in self.dsems.items() if d[1] > 0]
        for e in self.engs:
            for t in toks:
                if t.eng == e:
                    continue
                self._wait(e, t)
        self.bufs = {}

    def wait_all(self, e, bufs):
        for b in bufs:
            b = self.buf(b) if isinstance(b, str) else b
            self._wait(e, b.w)
            for r in b.r:
                self._wait(e, r)


IN_NAMES = []
DRAM_NAMES = {"o_y", "o_lat", "o_kr", "o_gla_p", "o_gla_s", "o_conv", "o_dbg", "kT_scr", "kpe_scr", "v_scr", "qn_scr", "qr_scr",
              "xT_scr", "xn_scr", "xo_scr", "Spre_scr", "hpre_scr"}


def _bf(a):
    return np.ascontiguousarray(a).astype(ml_dtypes.bfloat16)


def build_program():
    nc = bass.Bass("TRN2", target_bir_lowering=False)
    k = K(nc)

    def din(name, shape, dt=F32):
        IN_NAMES.append(name)
        return nc.dram_tensor(name, list(shape), dt, kind="ExternalInput").ap()

    def dout(name, shape, dt=F32):
        return nc.dram_tensor(name, list(shape), dt, kind="ExternalOutput").ap()

    def dscr(name, shape, dt):
        return nc.dram_tensor(name, list(shape), dt).ap()

    def sb(name, shape, dt):
        return nc.alloc_sbuf_tensor(name, list(shape), dt)

    x_pre = din("x_pre", [NPRE, D])
    x_win = din("x_win", [NT, D])
    pre_valid = din("pre_valid", [128, NPT])
    pre_bias = din("pre_bias", [128, NPT])
    cos_pre = din("cos_pre", [64, NPRE])
    sin_pre = din("sin_pre", [64, NPRE])
    cos_win = din("cos_win", [64, NT])
    sin_win = din("sin_win", [64, NT])
    w_in = din("w_in", [D, INC])
    w_kpe_rot = din("w_kpe_rot", [D, ROPE])
    w_a2 = din("w_a2", [16, 1024])
    b_a_bc = din("b_a_bc", [128, 1024])
    g_mix_bc = din("g_mix_bc", [128, D])
    g_gla_pk = din("g_gla_pk", [128, 4])
    g_q_pk = din("g_q_pk", [128, 4])
    g_kv_pk = din("g_kv_pk", [128, 4])
    g_ffn_pk = din("g_ffn_pk", [128, KC])
    fin_bc = din("fin_bc", [128, D])
    w_uq = din("w_uq", [QR, MH * (NOPE + ROPE)])
    w_uq_rot = din("w_uq_rot", [QR, MH * ROPE])
    w_uk = din("w_uk", [KVR, MH * NOPE])
    w_uv = din("w_uv", [KVR, MH * MV])
    w_o = din("w_o", [D, D])
    w_up = din("w_up", [D, 2 * DFF])
    conv_w_pk = din("conv_w_pk", [128, 3, FC])
    conv_b_pk = din("conv_b_pk", [128, FC])
    w_down = din("w_down", [DFF, D])
    c_lat = din("c_lat", [NSS, PAST, KVR])
    c_kr = din("c_kr", [NSS, PAST, ROPE])
    c_gla = din("c_gla", [NSS, GH, GDK, GDV])
    c_conv = din("c_conv", [NSS, 2, 2 * DFF])

    o_y = dout("o_y", [NT, D])
    o_lat = dout("o_lat", [NT, KVR])
    o_kr = dout("o_kr", [NT, ROPE])
    o_gla_p = dout("o_gla_p", [GH, GDK, GDV])
    o_gla_s = dout("o_gla_s", [NSS, GH, GDK, GDV])
    o_conv = dout("o_conv", [2 + 2 * NSS, 2 * DFF])


    c_ident = din("c_ident", [128, 128])
    c_ones = din("c_ones", [128, 128])
    c_ls = din("c_ls", [128, 128])
    c_ui = din("c_ui", [128, 128])

    NKC = NPRE + NT + NSS * PAST
    NVT = NPT + 9 + NSS + NSS * 8
    kT_scr = dscr("kT_scr", [MH, 128, NKC], BF16)
    kpe_scr = dscr("kpe_scr", [64, NKC], BF16)
    v_scr = dscr("v_scr", [NVT, 128, MH * MV], BF16)
    qn_scr = dscr("qn_scr", [MH, 128, NT], BF16)
    qr_scr = dscr("qr_scr", [MH, 64, NT], BF16)
    xT_scr = dscr("xT_scr", [KC, 128, NT], F32)
    xn_scr = dscr("xn_scr", [KC, 128, NT], F32)
    xo_scr = dscr("xo_scr", [KC, 128, NT], F32)
    hpre_scr = dscr("hpre_scr", [NPRE // 512, 128, KC * 512], BF16)

    ps = [nc.alloc_psum_tensor("ps%d" % i, [128, 512], F32) for i in range(8)]
    PB = ["ps%d" % i for i in range(8)]

    class Phase:
        n = 0

        def __init__(self):
            Phase.n += 1
            self.id = Phase.n
            self.st = ExitStack()

        def sb(self, name, shape, dt):
            return self.st.enter_context(nc.sbuf_tensor("p%d_%s" % (self.id, name), list(shape), dt))

        def close(self):
            k.barrier()
            self.st.close()

    def wslab(dst, src2d, name, q="pool"):
        k.dma(q, dst, src2d.rearrange("(kc p) c -> p kc c", p=128), writes=[name], sb=name)

    identf = sb("identf", [128, 128], F32)
    identb = sb("identb", [128, 128], BF16)
    onesf = sb("onesf", [128, 128], F32)
    onesb = sb("onesb", [128, 128], BF16)
    lsf = sb("lsf", [128, 128], F32)
    uif = sb("uif", [128, 128], F32)
    k.dma("sp", identf[:], c_ident, writes=["identf"], sb="identf")
    k.dma("pool", identb[:], c_ident, writes=["identb"], sb="identb")
    k.dma("sp", onesf[:], c_ones, writes=["onesf"], sb="onesf")
    k.dma("pool", onesb[:], c_ones, writes=["onesb"], sb="onesb")
    k.dma("sp", lsf[:], c_ls, writes=["lsf"], sb="lsf")
    k.dma("sp", uif[:], c_ui, writes=["uif"], sb="uif")
    epsc = sb("epsc", [128, 1], F32)
    one1 = sb("one1", [128, 1], F32)
    k.op("pool", lambda e: e.memset(epsc[:], EPS), writes=["epsc"])
    k.op("pool", lambda e: e.memset(one1[:], 1.0), writes=["one1"])
    hT = sb("hT", [128, KC, NT], BF16)
    GM = {}

    def load_gmix(ph_):
        GM["t"] = ph_.sb("gmix", [128, D], F32)
        k.dma("sp", GM["t"][:], g_mix_bc, writes=["gmix"], sb="gmix")
    CONSTS = ["identf", "identb", "onesf", "onesb", "lsf", "uif", "epsc", "one1", "gmix"]

    def rsqrt_inplace(ap_out, ap_in, n_div, rd, wr, epsap):
        k.op("act", lambda e: e.activation(out=ap_out, in_=ap_in, func=AF.Sqrt, scale=1.0 / n_div, bias=epsap),
             reads=rd + ["epsc"], writes=wr)
        k.op("dve", lambda e: e.reciprocal(out=ap_out, in_=ap_out), reads=wr, writes=wr)

    def norm_T_tile(src_rows, n, dst, dname, X, XN, xb, ss):
        k.dma("sp", X[0:n, :], src_rows, writes=[XN], sb=XN)
        k.op("act", lambda e: e.activation(out=xb[0:n, :], in_=X[0:n, :], func=AF.Square, accum_out=ss[0:n, 0:1]),
             reads=[XN], writes=["xb", "ss"])
        rsqrt_inplace(ss[0:n, 1:2], ss[0:n, 0:1], D, ["ss"], ["ss"], epsc[0:n, :])
        k.op("dve", lambda e: e.scalar_tensor_tensor(out=xb[0:n, :], in0=X[0:n, :], scalar=ss[0:n, 1:2], in1=GM["t"][0:n, :],
                                                     op0=ALU.mult, op1=ALU.mult), reads=[XN, "ss", "gmix"], writes=["xb"])
        for half in range(2):
            pT = ps[half][:].bitcast(BF16)
            for j in range(8):
                kc = half * 8 + j
                k.op("pe", lambda e: e.transpose(out=pT[:, j * 128:j * 128 + n], in_=xb[0:n, kc * 128:(kc + 1) * 128],
                                                 identity=identb[0:n, 0:n]),
                     reads=["xb", "identb"], writes=[PB[half]], mark=(j == 7))
            src = pT.rearrange("p (j t) -> p j t", j=8)[:, :, 0:n]
            if half == 0:
                k.op("dve", lambda e: e.tensor_copy(out=dst[:, 0:8, 0:n], in_=src), reads=[PB[half]], writes=[dname])
            else:
                k.op("act", lambda e: e.copy(out=dst[:, 8:16, 0:n], in_=src), reads=[PB[half]], writes=[dname])

    ph = Phase()
    load_gmix(ph)
    xt = [ph.sb("xt%d" % i, [128, D], F32) for i in range(2)]
    xb = ph.sb("xb", [128, D], BF16)
    ss = ph.sb("ss", [128, 2], F32)
    xts = ph.sb("xts", [128, KC, 128], F32)
    tiles = [(t0, min(128, NT - t0)) for t0 in range(0, NT, 128)]
    for ti, (t0, n) in enumerate(tiles):
        X, XN = xt[ti % 2], "xt%d" % (ti % 2)
        norm_T_tile(x_win[t0:t0 + n, :], n, hT[:, :, t0:t0 + n], "hT", X, XN, xb, ss)
        for g in range(4):
            for j in range(4):
                kc = g * 4 + j
                k.op("pe", lambda e: e.transpose(out=ps[2 + g][:, j * 128:j * 128 + n], in_=X[0:n, kc * 128:(kc + 1) * 128],
                                                 identity=identf[0:n, 0:n]),
                     reads=[XN, "identf"], writes=[PB[2 + g]], mark=(j == 3))
            src = ps[2 + g][:].rearrange("p (j t) -> p j t", j=4)[:, :, 0:n]
            k.op("dve" if g % 2 == 0 else "pool" if False else "dve",
                 lambda e: e.tensor_copy(out=xts[:, g * 4:g * 4 + 4, 0:n], in_=src), reads=[PB[2 + g]], writes=["xts"])
        k.dma("sp", xT_scr[:, :, t0:t0 + n].rearrange("kc p t -> p kc t"), xts[:, :, 0:n], reads=["xts"], writes=["xT_scr"], sb="xts")
    ph.close()

    NKV = KVR + ROPE

    def mla_block(P, hsrc, hname, n, cosap, sinap, cname, latb_dst, kcol0, out_row0):
        wkv, wrot, sq, rstd, lat32, kr32, tmpk, kpeb, ostage, gkv = (P[x] for x in
            ("wkv", "wrot", "sq", "rstd", "lat32", "kr32", "tmpk", "kpeb", "ostage", "gkv"))
        for c in range(4):
            for kc in range(KC):
                k.op("pe", lambda e: e.matmul(ps[2 + c][:, 0:n], lhsT=wkv[:, kc, c * 128:(c + 1) * 128], rhs=hsrc[:, kc, :],
                                              start=(kc == 0), stop=(kc == KC - 1)),
                     reads=["wkv", hname], writes=[PB[2 + c]], mark=(kc == KC - 1))
            k.op("act", lambda e: e.activation(out=sq[:, c, 0:n], in_=ps[2 + c][:, 0:n], func=AF.Square), reads=[PB[2 + c]], writes=["sq"])
        for c in range(4):
            k.op("pe", lambda e: e.matmul(ps[6][:, 0:n], lhsT=onesf[:, :], rhs=sq[:, c, 0:n], start=(c == 0), stop=(c == 3)),
                 reads=["onesf", "sq"], writes=[PB[6]], mark=(c == 3))
        rsqrt_inplace(rstd[:, 0:n], ps[6][:, 0:n], KVR, [PB[6]], ["rstd"], epsc[:, :])
        for c in range(4):
            k.op("dve", lambda e: e.scalar_tensor_tensor(out=lat32[:, c, 0:n], in0=ps[2 + c][:, 0:n], scalar=gkv[:, c:c + 1],
                                                         in1=rstd[:, 0:n], op0=ALU.mult, op1=ALU.mult),
                 reads=[PB[2 + c], "gkv", "rstd"], writes=["lat32"])
        k.op("act", lambda e: e.copy(out=latb_dst[:, :, 0:n], in_=lat32[:, :, 0:n]), reads=["lat32"], writes=["latb"])
        for kc in range(KC):
            k.op("pe", lambda e: e.matmul(ps[7][:, 0:n], lhsT=wrot[:, kc, 0:128], rhs=hsrc[:, kc, :],
                                          start=(kc == 0), stop=(kc == KC - 1)),
                 reads=["wrot", hname], writes=[PB[7]], mark=(kc == KC - 1))
        k.op("dve", lambda e: e.tensor_tensor(out=kr32[:, 0:n], in0=ps[7][0:64, 0:n], in1=cosap, op=ALU.mult),
             reads=[PB[7], cname], writes=["kr32"])
        for kc in range(KC):
            k.op("pe", lambda e: e.matmul(ps[7][:, 0:n], lhsT=wrot[:, kc, 128:256], rhs=hsrc[:, kc, :], start=(kc == 0), stop=(kc == KC - 1)),
                 reads=["wrot", hname], writes=[PB[7]], mark=(kc == KC - 1))
        k.op("dve", lambda e: e.tensor_tensor(out=tmpk[:, 0:n], in0=ps[7][0:64, 0:n], in1=sinap, op=ALU.mult),
             reads=[PB[7], cname], writes=["tmpk"])
        k.op("dve", lambda e: e.tensor_tensor(out=kr32[:, 0:n], in0=kr32[:, 0:n], in1=tmpk[:, 0:n], op=ALU.add),
             reads=["kr32", "tmpk"], writes=["kr32"])
        k.op("act", lambda e: e.copy(out=kpeb[:, 0:n], in_=kr32[:, 0:n]), reads=["kr32"], writes=["kpeb"])
        k.dma("sp", kpe_scr[:, kcol0:kcol0 + n], kpeb[:, 0:n], reads=["kpeb"], writes=["kpe_scr"], sb="kpeb")
        if out_row0 is not None:
            for s0 in range(0, n, 128):
                m = min(128, n - s0)
                for c in range(4):
                    k.op("pe", lambda e: e.transpose(out=ps[0][0:m, c * 128:(c + 1) * 128], in_=lat32[:, c, s0:s0 + m], identity=identf[:, :]),
                         reads=["lat32", "identf"], writes=[PB[0]], mark=(c == 3))
                k.op("pe", lambda e: e.transpose(out=ps[1][0:m, 0:64], in_=kr32[0:64, s0:s0 + m], identity=identf[0:64, 0:64]),
                     reads=["kr32", "identf"], writes=[PB[1]])
                k.op("dve", lambda e: e.tensor_copy(out=ostage[0:m, 0:KVR], in_=ps[0][0:m, :]), reads=[PB[0]], writes=["ostage"])
                k.op("act", lambda e: e.copy(out=ostage[0:m, KVR:NKV], in_=ps[1][0:m, 0:64]), reads=[PB[1]], writes=["ostage"])
                r0 = out_row0 + s0
                k.dma("sp", o_lat[r0:r0 + m, :], ostage[0:m, 0:KVR], reads=["ostage"], writes=["o_lat"], sb="ostage")
                k.dma("sp", o_kr[r0:r0 + m, :], ostage[0:m, KVR:NKV], reads=["ostage"], writes=["o_kr"], sb="ostage_b")

    kgen_i = [0]
    vgen_i = [0]

    def k_gen(P, latb, n, kcol0):
        wuk, kst = P["wuk"], P["kst"]
        for h in range(MH):
            b = 2 + (kgen_i[0] % 4)
            si = kgen_i[0] % len(kst)
            kgen_i[0] += 1
            for c in range(4):
                k.op("pe", lambda e: e.matmul(ps[b][:, 0:n], lhsT=wuk[:, c, h * NOPE:(h + 1) * NOPE], rhs=latb[:, c, 0:n],
                                              start=(c == 0), stop=(c == 3)), reads=["wuk", "latb"], writes=[PB[b]], mark=(c == 3))
            if si % 2 == 0:
                k.op("dve", lambda e: e.tensor_copy(out=kst[si][:, 0:n], in_=ps[b][:, 0:n]), reads=[PB[b]], writes=["kst%d" % si])
            else:
                k.op("act", lambda e: e.copy(out=kst[si][:, 0:n], in_=ps[b][:, 0:n]), reads=[PB[b]], writes=["kst%d" % si])
            k.dma("pool", kT_scr[h, :, kcol0:kcol0 + n], kst[si][:, 0:n], reads=["kst%d" % si], writes=["kT_scr"], sb="kst%d" % si)

    def v_gen(P, latb, t0, m, vt):
        wuv = P["wuv"]
        vi = vgen_i[0] % len(P["vst"])
        vgen_i[0] += 1
        vst, vsn = P["vst"][vi], "vst%d" % vi
        for cg in range(4):
            b = 2 + cg
            for c in range(4):
                k.op("pe", lambda e: e.matmul(ps[b][0:m, :], lhsT=latb[:, c, t0:t0 + m], rhs=wuv[:, c, cg * 512:(cg + 1) * 512],
                                              start=(c == 0), stop=(c == 3)), reads=["wuv", "latb"], writes=[PB[b]], mark=(c == 3))
            if cg % 2 == 0:
                k.op("dve", lambda e: e.tensor_copy(out=vst[0:m, cg * 512:(cg + 1) * 512], in_=ps[b][0:m, :]), reads=[PB[b]], writes=[vsn])
            else:
                k.op("act", lambda e: e.copy(out=vst[0:m, cg * 512:(cg + 1) * 512], in_=ps[b][0:m, :]), reads=[PB[b]], writes=[vsn])
        k.dma("pool", v_scr[vt, 0:m, :], vst[0:m, :], reads=[vsn], writes=["v_scr"], sb=vsn)

    def mla_phase_tensors(ph, n_lat):
        P = {}
        P["wkv"] = ph.sb("wkv", [128, KC, NKV], BF16)
        P["wrot"] = ph.sb("wrot", [128, KC, 256], BF16)
        k.op("pool", lambda e: e.memset(P["wrot"][:], 0.0), writes=["wrot"])
        wslab(P["wkv"][:], w_in[:, O_CKV:O_CKV + NKV], "wkv")
        wslab(P["wrot"][:, :, 0:ROPE], w_in[:, O_KPE:O_KPE + ROPE], "wrot")
        wslab(P["wrot"][:, :, 128:128 + ROPE], w_kpe_rot, "wrot")
        P["wuk"] = ph.sb("wuk", [128, 4, MH * NOPE], BF16)
        P["wuv"] = ph.sb("wuv", [128, 4, MH * MV], BF16)
        wslab(P["wuk"][:], w_uk, "wuk")
        wslab(P["wuv"][:], w_uv, "wuv")
        P["gkv"] = ph.sb("gkv", [128, 4], F32)
        k.dma("sp", P["gkv"][:], g_kv_pk, writes=["gkv"], sb="gkv")
        P["sq"] = ph.sb("sq", [128, 4, 512], F32)
        P["rstd"] = ph.sb("rstd", [128, 512], F32)
        P["lat32"] = ph.sb("lat32", [128, 4, 512], F32)
        P["kr32"] = ph.sb("kr32", [64, 512], F32)
        P["tmpk"] = ph.sb("tmpk", [64, 512], F32)
        P["kpeb"] = ph.sb("kpeb", [64, 512], BF16)
        P["ostage"] = ph.sb("ostage", [128, NKV], F32) if n_lat == NT else None
        P["kst"] = [ph.sb("kst%d" % i, [128, 512], BF16) for i in range(4)]
        P["vst"] = [ph.sb("vst%d" % i, [128, MH * MV], BF16) for i in range(2)]
        P["latb"] = ph.sb("latb", [128, 4, n_lat], BF16)
        return P

    ph = Phase()
    P = mla_phase_tensors(ph, NT)
    cosw = ph.sb("cosw", [64, NT], F32)
    sinw = ph.sb("sinw", [64, NT], F32)
    k.dma("sp", cosw[:], cos_win, writes=["cosw"], sb="cosw")
    k.dma("sp", sinw[:], sin_win, writes=["cosw"], sb="cosw")
    for p0 in (range(0, NT, 512) if K_P2 else []):
        n = min(512, NT - p0)
        mla_block(P, hT[:, :, p0:p0 + n], "hT", n, cosw[:, p0:p0 + n], sinw[:, p0:p0 + n], "cosw",
                  P["latb"][:, :, p0:p0 + n], NPRE + p0, p0)
        k_gen(P, P["latb"][:, :, p0:p0 + n], n, NPRE + p0)
    wtiles = [(0, 16)] + [(16 + 128 * m, 128) for m in range(8)] + [(WINP + TS * s_, TS) for s_ in range(NSS)]
    for wi, (t0, m) in (enumerate(wtiles) if K_P2 >= 2 else []):
        v_gen(P, P["latb"], t0, m, NPT + wi)
    ph.close()

    ph = Phase()
    load_gmix(ph)
    P = mla_phase_tensors(ph, 512)
    xt = [ph.sb("xt%d" % i, [128, D], F32) for i in range(2)]
    xb = ph.sb("xb", [128, D], BF16)
    ss = ph.sb("ss", [128, 2], F32)
    hTp = ph.sb("hTp", [128, KC, 512], BF16)
    cosp = ph.sb("cosp", [64, 512], F32)
    sinp = ph.sb("sinp", [64, 512], F32)
    for blk in range(NPRE_BLK):
        b0 = blk * 512
        k.dma("sp", cosp[:], cos_pre[:, b0:b0 + 512], writes=["cosp"], sb="cosp")
        k.dma("sp", sinp[:], sin_pre[:, b0:b0 + 512], writes=["cosp"], sb="cosp")
        for j in range(4):
            ti = blk * 4 + j
            norm_T_tile(x_pre[ti * 128:(ti + 1) * 128, :], 128, hTp[:, :, j * 128:(j + 1) * 128], "hTp", xt[ti % 2], "xt%d" % (ti % 2), xb, ss)
        k.dma("pool", hpre_scr[blk], hTp[:].rearrange("p kc t -> p (kc t)"), reads=["hTp"], writes=["hpre_scr"], sb="hTp")
        mla_block(P, hTp[:, :, :], "hTp", 512, cosp[:, :], sinp[:, :], "cosp", P["latb"][:, :, :], b0, None)
        k_gen(P, P["latb"], 512, b0)
        for j in range(4):
            v_gen(P, P["latb"], j * 128, 128, blk * 4 + j)
    ph.close()

    ph = Phase()
    P = {}
    P["wuk"] = ph.sb("wuk", [128, 4, MH * NOPE], BF16)
    P["wuv"] = ph.sb("wuv", [128, 4, MH * MV], BF16)
    wslab(P["wuk"][:], w_uk, "wuk")
    wslab(P["wuv"][:], w_uv, "wuv")
    P["kst"] = [ph.sb("kst%d" % i, [128, 512], BF16) for i in range(4)]
    P["vst"] = [ph.sb("vst%d" % i, [128, MH * MV], BF16) for i in range(2)]
    P["latb"] = ph.sb("latb", [128, 4, PAST], BF16)
    ctm = [ph.sb("ctm%d" % i, [128, NKV + 64], BF16) for i in range(2)]
    for i in range(2):
        k.op("pool", lambda e: e.memset(ctm[i][:], 0.0), writes=["ctm%d" % i])
    kpec = ph.sb("kpec", [64, PAST], BF16)
    ctf = [ph.sb("ctf%d" % i, [128, NKV], F32) for i in range(2)]
    for s_ in range(NSS_RUN):
        kc0 = NPRE + NT + s_ * PAST
        for j in range(PAST // 128):
            T, TN = ctm[j % 2], "ctm%d" % (j % 2)
            CF, CFN = ctf[j % 2], "ctf%d" % (j % 2)
            k.dma("sp", CF[:, 0:KVR], c_lat[s_, j * 128:(j + 1) * 128, :], writes=[CFN], sb=CFN)
            k.dma("sp", CF[:, KVR:NKV], c_kr[s_, j * 128:(j + 1) * 128, :], writes=[CFN], sb=CFN)
            if not (K_3B & 16):
                continue
            k.op("dve", lambda e: e.tensor_copy(out=T[:, 0:NKV], in_=CF[:, :]), reads=[CFN], writes=[TN])
            if not (K_3B & 32):
                continue
            pT = ps[j % 2][:].bitcast(BF16)
            for c in range(4):
                k.op("pe", lambda e: e.transpose(out=pT[:, c * 128:(c + 1) * 128], in_=T[:, c * 128:(c + 1) * 128], identity=identb[:, :]),
                     reads=[TN, "identb"], writes=[PB[j % 2]], mark=(c == 3))
            if K_3B & 64:
                k.op("pe", lambda e: e.transpose(out=pT[:, 512:640], in_=T[:, 512:640], identity=identb[:, :]),
                     reads=[TN, "identb"], writes=[PB[j % 2]])
            k.op("dve", lambda e: e.tensor_copy(out=P["latb"][:, :, j * 128:(j + 1) * 128],
                                                in_=pT[:, 0:512].rearrange("p (c t) -> p c t", c=4)), reads=[PB[j % 2]], writes=["latb"])
            if K_3B & 64:
                k.op("dve", lambda e: e.tensor_copy(out=kpec[:, j * 128:(j + 1) * 128], in_=pT[0:64, 512:640]), reads=[PB[j % 2]], writes=["kpec"])
        if K_3B & 2:
            k.dma("sp", kpe_scr[:, kc0:kc0 + PAST], kpec[:, :], reads=["kpec"], writes=["kpe_scr"], sb="kpec")
        for p0 in (range(0, PAST, 512) if K_3B & 4 else []):
            k_gen(P, P["latb"][:, :, p0:p0 + 512], 512, kc0 + p0)
        for j in (range(PAST // 128) if K_3B & 8 else []):
            v_gen(P, P["latb"], j * 128, 128, NPT + 9 + NSS + s_ * 8 + j)
    ph.close()

    Spre_scr = dscr("Spre_scr", [GH, GDK, GDV], F32)

    def gla_common(ph):
        G = {}
        G["bab"] = ph.sb("bab", [128, 1024], F32)
        k.dma("sp", G["bab"][:], b_a_bc, writes=["bab"], sb="bab")
        G["wa"] = ph.sb("wa", [128, KC, 128], BF16)
        k.op("pool", lambda e: e.memset(G["wa"][:], 0.0), writes=["wa"])
        wslab(G["wa"][:, :, 0:16], w_in[:, O_A:O_A + 16], "wa")
        G["wa2b"] = ph.sb("wa2b", [128, 1024], BF16)
        k.op("pool", lambda e: e.memset(G["wa2b"][:], 0.0), writes=["wa2b"])
        k.dma("pool", G["wa2b"][0:16, :], w_a2, writes=["wa2b"], sb="wa2b")
        G["agT"] = ph.sb("agT", [128, 128], BF16)
        for nm, shp, dt_ in (("zz", [128, 256], F32), ("la", [128, 256], F32), ("er", [128, 256], F32), ("dec", [128, 2], F32),
                             ("kend", [128, 256], BF16), ("vb", [128, 512], BF16)):
            G[nm] = [ph.sb("%s%d" % (nm, i), shp, dt_) for i in range(2)]
        return G

    def gla_ag(G, hsrc, hname, C):
        for kc in range(KC):
            k.op("pe", lambda e: e.matmul(ps[0][:, 0:C], lhsT=G["wa"][:, kc, :], rhs=hsrc[:, kc, :], start=(kc == 0), stop=(kc == KC - 1)),
                 reads=["wa", hname], writes=[PB[0]], mark=(kc == KC - 1))
        k.op("act", lambda e: e.copy(out=G["agT"][:, 0:C], in_=ps[0][:, 0:C]), reads=[PB[0]], writes=["agT"])

    def gla_state_step(G, hsrc, hname, C, h, wk_t, wkn, wv_t, wvn, S_t, SN, vcol, mid=None, slot=0, stages="ABC"):
        zz, la, er, dec, kend, vb = (G[x][slot] for x in ("zz", "la", "er", "dec", "kend", "vb"))
        zzn, lan, ern, decn, kendn, vbn = ("%s%d" % (x, slot) for x in ("zz", "la", "er", "dec", "kend", "vb"))
        BK, BV, BR = (1, 2, 3) if slot == 0 else (4, 5, 6)
        if "A" in stages:
            gla_stage_a(G, hsrc, hname, C, h, wk_t, wkn, wv_t, wvn, vcol, slot, BK, BV)
        if "B" in stages:
            gla_stage_b(G, C, vcol, slot, BK, BV, BR)
            if mid is not None:
                mid()
        if "C" in stages:
            gla_stage_c(G, C, S_t, SN, slot)

    def gla_stage_a(G, hsrc, hname, C, h, wk_t, wkn, wv_t, wvn, vcol, slot, BK, BV):
        zz, la, er, dec, kend, vb = (G[x][slot] for x in ("zz", "la", "er", "dec", "kend", "vb"))
        zzn, lan, ern, decn, kendn, vbn = ("%s%d" % (x, slot) for x in ("zz", "la", "er", "dec", "kend", "vb"))
        for kc in range(KC):
            k.op("pe", lambda e: e.matmul(ps[BK][0:C, 0:GDK], lhsT=hsrc[:, kc, :], rhs=wk_t[:, kc, :], start=(kc == 0), stop=(kc == KC - 1)),
                 reads=[hname, wkn], writes=[PB[BK]], mark=(kc == KC - 1))
        for kc in range(KC):
            k.op("pe", lambda e: e.matmul(ps[BV][0:C, 0:GDV], lhsT=hsrc[:, kc, :], rhs=wv_t[:, kc, :], start=(kc == 0), stop=(kc == KC - 1)),
                 reads=[hname, wvn], writes=[PB[BV]], mark=(kc == KC - 1))
        k.op("pe", lambda e: e.matmul(ps[BK][0:C, GDK:2 * GDK], lhsT=G["agT"][:, 0:C], rhs=G["wa2b"][:, h * GDK:(h + 1) * GDK], start=True, stop=True),
             reads=["agT", "wa2b"], writes=[PB[BK]])
        k.op("dve", lambda e: e.tensor_tensor(out=zz[0:C, :], in0=ps[BK][0:C, GDK:2 * GDK], in1=G["bab"][0:C, h * GDK:(h + 1) * GDK], op=ALU.add),
             reads=[PB[BK], "bab"], writes=[zzn])
        k.op("act", lambda e: e.activation(out=zz[0:C, :], in_=zz[0:C, :], func=AF.Exp, scale=-1.0), reads=[zzn], writes=[zzn])
        k.op("act", lambda e: e.activation(out=zz[0:C, :], in_=zz[0:C, :], func=AF.Ln, bias=one1[0:C, :]), reads=[zzn, "one1"], writes=[zzn])
        k.op("dve", lambda e: e.tensor_scalar(out=la[0:C, :], in0=zz[0:C, :], scalar1=-1.0 / 16.0, scalar2=-5.0, op0=ALU.mult, op1=ALU.max),
             reads=[zzn], writes=[lan])
        if vcol is not None:
            k.op("dve", lambda e: e.tensor_scalar(out=la[0:C, :], in0=la[0:C, :], scalar1=vcol, scalar2=None, op0=ALU.mult),
                 reads=[lan, "pval"], writes=[lan])

    def gla_stage_b(G, C, vcol, slot, BK, BV, BR):
        zz, la, er, dec, kend, vb = (G[x][slot] for x in ("zz", "la", "er", "dec", "kend", "vb"))
        zzn, lan, ern, decn, kendn, vbn = ("%s%d" % (x, slot) for x in ("zz", "la", "er", "dec", "kend", "vb"))
        k.op("pe", lambda e: e.matmul(ps[BR][0:C, 0:GDK], lhsT=lsf[0:C, 0:C], rhs=la[0:C, :], start=True, stop=True),
             reads=["lsf", lan], writes=[PB[BR]])
        for dc in range(2):
            k.op("pe", lambda e: e.matmul(ps[BR][:, GDK + dc:GDK + dc + 1], lhsT=la[0:C, dc * 128:(dc + 1) * 128], rhs=onesf[0:C, 0:1], start=True, stop=True),
                 reads=[lan, "onesf"], writes=[PB[BR]], mark=(dc == 1))
        k.op("act", lambda e: e.activation(out=er[0:C, :], in_=ps[BR][0:C, 0:GDK], func=AF.Exp), reads=[PB[BR]], writes=[ern])
        k.op("act", lambda e: e.activation(out=dec[:, :], in_=ps[BR][:, GDK:GDK + 2], func=AF.Exp), reads=[PB[BR]], writes=[decn])
        if vcol is not None:
            k.op("dve", lambda e: e.scalar_tensor_tensor(out=kend[0:C, :], in0=ps[BK][0:C, 0:GDK], scalar=vcol, in1=er[0:C, :],
                                                         op0=ALU.mult, op1=ALU.mult), reads=[PB[BK], ern, "pval"], writes=[kendn])
        else:
            k.op("dve", lambda e: e.tensor_tensor(out=kend[0:C, :], in0=ps[BK][0:C, 0:GDK], in1=er[0:C, :], op=ALU.mult),
                 reads=[PB[BK], ern], writes=[kendn])
        k.op("act", lambda e: e.copy(out=vb[0:C, :], in_=ps[BV][0:C, 0:GDV]), reads=[PB[BV]], writes=[vbn])

    def gla_stage_c(G, C, S_t, SN, slot):
        zz, la, er, dec, kend, vb = (G[x][slot] for x in ("zz", "la", "er", "dec", "kend", "vb"))
        zzn, lan, ern, decn, kendn, vbn = ("%s%d" % (x, slot) for x in ("zz", "la", "er", "dec", "kend", "vb"))
        for dc in range(2):
            BD = 7 if dc == 0 else 0
            k.op("pe", lambda e: e.matmul(ps[BD][:, 0:GDV], lhsT=kend[0:C, dc * 128:(dc + 1) * 128], rhs=vb[0:C, :], start=True, stop=True),
                 reads=[kendn, vbn], writes=[PB[BD]])
            k.op("dve", lambda e: e.scalar_tensor_tensor(out=S_t[:, dc, :], in0=S_t[:, dc, :], scalar=dec[:, dc:dc + 1],
                                                         in1=ps[BD][:, 0:GDV], op0=ALU.mult, op1=ALU.add),
                 reads=[SN, decn, PB[BD]], writes=[SN])

    ph = Phase()
    G = gla_common(ph)
    hTps = [ph.sb("hTp%d" % i, [128, KC, 512], BF16) for i in range(2)]
    pval = ph.sb("pval", [128, NPT], F32)
    k.dma("sp", pval[:], pre_valid, writes=["pval"], sb="pval")
    wk = [ph.sb("wk%d" % i, [128, KC, GDK], BF16) for i in range(2)]
    wv = [ph.sb("wv%d" % i, [128, KC, GDV], BF16) for i in range(2)]
    S32 = [ph.sb("S32_%d" % i, [128, 2, GDV], F32) for i in range(2)]
    for pair in range(2):
        for hi in range(2):
            h = pair * 2 + hi
            wslab(wk[hi][:], w_in[:, O_K + h * GDK:O_K + (h + 1) * GDK], "wk%d" % hi)
            wslab(wv[hi][:], w_in[:, O_V + h * GDV:O_V + (h + 1) * GDV], "wv%d" % hi)
            k.op("pool", lambda e: e.memset(S32[hi][:], 0.0), writes=["S32_%d" % hi])
        for ti in range(NPT_RUN):
            blk, jt = ti // 4, ti % 4
            hTn = "hTp%d" % (blk % 2)
            if jt == 0:
                k.dma("sp", hTps[blk % 2][:].rearrange("p kc t -> p (kc t)"), hpre_scr[blk], writes=[hTn], sb=hTn)
            hTp = hTps[blk % 2][:, :, jt * 128:(jt + 1) * 128]
            gla_ag(G, hTp, hTn, 128)
            for stg in "ABC":
                for hi in range(2):
                    gla_state_step(G, hTp, hTn, 128, pair * 2 + hi, wk[hi], "wk%d" % hi, wv[hi], "wv%d" % hi,
                                   S32[hi], "S32_%d" % hi, pval[:, ti:ti + 1], slot=hi, stages=stg)
        for hi in range(2):
            h = pair * 2 + hi
            k.dma("sp", Spre_scr[h].rearrange("(dc p) e -> p dc e", p=128), S32[hi][:], reads=["S32_%d" % hi], writes=["Spre_scr"], sb="S32_%d" % hi)
    ph.close()

    mergedT = sb("mergedT", [128, KC, NT], BF16)
    ph = Phase()
    G = gla_common(ph)
    ggla = ph.sb("ggla", [128, 4], F32)
    k.dma("sp", ggla[:], g_gla_pk, writes=["ggla"], sb="ggla")
    wk1 = ph.sb("wk", [128, KC, GDK], BF16)
    wv1 = ph.sb("wv", [128, KC, GDV], BF16)
    wq_t = ph.sb("wq", [128, KC, GDK], BF16)
    wr_t = ph.sb("wr", [128, KC, GDV], BF16)
    wga_t = ph.sb("wga", [128, KC, GDV], BF16)
    S1 = ph.sb("S1", [128, 2, GDV], F32)
    Sb = ph.sb("Sb", [128, 2, GDV], BF16)
    qTh = ph.sb("qTh", [128, 2, NT], BF16)
    kTf = ph.sb("kTf", [128, 2, NT], BF16)
    rT = ph.sb("rT", [128, 4, NT], BF16)
    srT = ph.sb("srT", [128, 4, NT], BF16)
    gtmp = ph.sb("gtmp", [128, 512], F32)
    eb = ph.sb("eb", [128, 2, 128], F32)
    enb = ph.sb("enb", [128, 2, 128], F32)
    qd = ph.sb("qd", [128, 2, 128], BF16)
    ki = ph.sb("ki", [128, 2, 128], BF16)
    attb = ph.sb("attb", [128, 128], BF16)
    osq = ph.sb("osq", [128, 512], F32)
    orst = ph.sb("orst", [128, 128], F32)
    otmp = ph.sb("otmp", [128, 128], F32)
    pieces = [(p0, min(512, NT - p0)) for p0 in range(0, NT, 512)]
    gemm_i = [0]

    def gemm_fm(wt, wname, col0, post):
        for (p0, n) in pieces:
            b = 5 + gemm_i[0] % 3
            gemm_i[0] += 1
            for kc in range(KC):
                k.op("pe", lambda e: e.matmul(ps[b][:, 0:n], lhsT=wt[:, kc, col0:col0 + 128], rhs=hT[:, kc, p0:p0 + n],
                                              start=(kc == 0), stop=(kc == KC - 1)), reads=[wname, "hT"], writes=[PB[b]], mark=(kc == KC - 1))
            post(ps[b][:, 0:n], PB[b], p0, n)

    def gla_out(C, c0, h):
        la = G["la"][0]
        vb = G["vb"][0]
        for dc in range(2):
            k.op("pe", lambda e: e.matmul(ps[3][:, dc * C:(dc + 1) * C], lhsT=la[0:C, dc * 128:(dc + 1) * 128], rhs=uif[0:C, 0:C], start=True, stop=True),
                 reads=["la0", "uif"], writes=[PB[3]], mark=(dc == 1))
        pv = ps[3][:, 0:2 * C].rearrange("p (d t) -> p d t", d=2)
        k.op("act", lambda e: e.activation(out=eb[:, :, 0:C], in_=pv, func=AF.Exp), reads=[PB[3]], writes=["eb"])
        k.op("act", lambda e: e.activation(out=enb[:, :, 0:C], in_=pv, func=AF.Exp, scale=-1.0), reads=[PB[3]], writes=["enb"])
        k.op("dve", lambda e: e.tensor_tensor(out=qd[:, :, 0:C], in0=qTh[:, :, c0:c0 + C], in1=eb[:, :, 0:C], op=ALU.mult),
             reads=["qTh", "eb"], writes=["qd"])
        k.op("dve", lambda e: e.tensor_tensor(out=ki[:, :, 0:C], in0=kTf[:, :, c0:c0 + C], in1=enb[:, :, 0:C], op=ALU.mult),
             reads=["kTf", "enb"], writes=["ki"])
        for dc in range(2):
            k.op("pe", lambda e: e.matmul(ps[0][0:C, 0:C], lhsT=ki[:, dc, 0:C], rhs=qd[:, dc, 0:C], start=(dc == 0), stop=(dc == 1)),
                 reads=["ki", "qd"], writes=[PB[0]], mark=(dc == 1))
        k.op("dve", lambda e: e.tensor_tensor(out=attb[0:C, 0:C], in0=ps[0][0:C, 0:C], in1=uif[0:C, 0:C], op=ALU.mult),
             reads=[PB[0], "uif"], writes=["attb"])
        for ec in range(4):
            k.op("pe", lambda e: e.matmul(ps[1][:, ec * C:(ec + 1) * C], lhsT=vb[0:C, ec * 128:(ec + 1) * 128], rhs=attb[0:C, 0:C], start=True, stop=False),
                 reads=["vb0", "attb"], writes=[PB[1]], mark=False)
            for dc in range(2):
                k.op("pe", lambda e: e.matmul(ps[1][:, ec * C:(ec + 1) * C], lhsT=Sb[:, dc, ec * 128:(ec + 1) * 128], rhs=qd[:, dc, 0:C],
                                              start=False, stop=(dc == 1)), reads=["Sb", "qd"], writes=[PB[1]], mark=(dc == 1 and ec == 3))
        k.op("act", lambda e: e.activation(out=osq[:, 0:4 * C], in_=ps[1][:, 0:4 * C], func=AF.Square), reads=[PB[1]], writes=["osq"])
        for ec in range(4):
            k.op("pe", lambda e: e.matmul(ps[4][:, 0:C], lhsT=onesf[:, :], rhs=osq[:, ec * C:(ec + 1) * C], start=(ec == 0), stop=(ec == 3)),
                 reads=["onesf", "osq"], writes=[PB[4]], mark=(ec == 3))
        rsqrt_inplace(orst[:, 0:C], ps[4][:, 0:C], GDV, [PB[4]], ["orst"], epsc[:, :])
        for ec in range(4):
            k.op("dve", lambda e: e.scalar_tensor_tensor(out=otmp[:, 0:C], in0=ps[1][:, ec * C:(ec + 1) * C], scalar=ggla[:, ec:ec + 1],
                                                         in1=orst[:, 0:C], op0=ALU.mult, op1=ALU.mult), reads=[PB[1], "ggla", "orst"], writes=["otmp"])
            k.op("dve", lambda e: e.tensor_tensor(out=mergedT[:, 4 * h + ec, c0:c0 + C], in0=otmp[:, 0:C], in1=srT[:, ec, c0:c0 + C], op=ALU.mult),
                 reads=["otmp", "srT"], writes=["mergedT"])

    for h in range(GH_RUN):
        wslab(wk1[:], w_in[:, O_K + h * GDK:O_K + (h + 1) * GDK], "wk")
        wslab(wv1[:], w_in[:, O_V + h * GDV:O_V + (h + 1) * GDV], "wv")
        k.dma("sp", S1[:], Spre_scr[h].rearrange("(dc p) e -> p dc e", p=128), reads=["Spre_scr"], writes=["S1"], sb="S1")
        wslab(wq_t[:], w_in[:, O_Q + h * GDK:O_Q + (h + 1) * GDK], "wq")
        wslab(wr_t[:], w_in[:, O_R + h * GDV:O_R + (h + 1) * GDV], "wr")
        wslab(wga_t[:], w_in[:, O_GA + h * GDV:O_GA + (h + 1) * GDV], "wga")
        for dc in range(2):
            gemm_fm(wq_t, "wq", dc * 128, lambda pa, pn, p0, n, dc=dc: k.op(
                "act", lambda e: e.activation(out=qTh[:, dc, p0:p0 + n], in_=pa, func=AF.Copy, scale=GDK ** -0.5), reads=[pn], writes=["qTh"]))
            gemm_fm(wk1, "wk", dc * 128, lambda pa, pn, p0, n, dc=dc: k.op(
                "dve", lambda e: e.tensor_copy(out=kTf[:, dc, p0:p0 + n], in_=pa), reads=[pn], writes=["kTf"]))
        for ec in range(4):
            gemm_fm(wr_t, "wr", ec * 128, lambda pa, pn, p0, n, ec=ec: k.op(
                "act", lambda e: e.activation(out=rT[:, ec, p0:p0 + n], in_=pa, func=AF.Silu), reads=[pn], writes=["rT"]))
        for ec in range(4):
            def post_ga(pa, pn, p0, n, ec=ec):
                k.op("act", lambda e: e.activation(out=gtmp[:, 0:n], in_=pa, func=AF.Sigmoid), reads=[pn], writes=["gtmp"])
                k.op("dve", lambda e: e.tensor_tensor(out=srT[:, ec, p0:p0 + n], in0=rT[:, ec, p0:p0 + n], in1=gtmp[:, 0:n], op=ALU.mult),
                     reads=["rT", "gtmp"], writes=["srT"])
            gemm_fm(wga_t, "wga", ec * 128, post_ga)
        chunks = [(0, 16)] + [(16 + 128 * m, 128) for m in range(8)]
        for (c0, C) in chunks:
            gla_ag(G, hT[:, :, c0:c0 + C], "hT", C)
            k.op("act", lambda e: e.copy(out=Sb[:], in_=S1[:]), reads=["S1"], writes=["Sb"])
            gla_state_step(G, hT[:, :, c0:c0 + C], "hT", C, h, wk1, "wk", wv1, "wv", S1, "S1", None,
                           mid=lambda C=C, c0=c0: gla_out(C, c0, h))
        k.dma("sp", o_gla_p[h].rearrange("(dc p) e -> p dc e", p=128), S1[:], reads=["S1"], writes=["o_gla_p"], sb="S1")
        for s_ in range(NSS):
            c0 = WINP + TS * s_
            k.dma("sp", S1[:], c_gla[s_, h].rearrange("(dc p) e -> p dc e", p=128), writes=["S1"], sb="S1")
            gla_ag(G, hT[:, :, c0:c0 + TS], "hT", TS)
            k.op("act", lambda e: e.copy(out=Sb[:], in_=S1[:]), reads=["S1"], writes=["Sb"])
            gla_state_step(G, hT[:, :, c0:c0 + TS], "hT", TS, h, wk1, "wk", wv1, "wv", S1, "S1", None,
                           mid=lambda c0=c0: gla_out(TS, c0, h))
            k.dma("sp", o_gla_s[s_, h].rearrange("(dc p) e -> p dc e", p=128), S1[:], reads=["S1"], writes=["o_gla_s"], sb="S1")
    ph.close()

    ph = Phase()
    wcq = ph.sb("wcq", [128, KC, QR], BF16)
    wslab(wcq[:], w_in[:, O_CQ:O_CQ + QR], "wcq")
    wuq = ph.sb("wuq", [128, 4, MH * (NOPE + ROPE)], BF16)
    wslab(wuq[:], w_uq, "wuq")
    wuqr = ph.sb("wuqr", [128, 4, MH * ROPE], BF16)
    wslab(wuqr[:], w_uq_rot, "wuqr")
    gq = ph.sb("gq", [128, 4], F32)
    k.dma("sp", gq[:], g_q_pk, writes=["gq"], sb="gq")
    cosw = ph.sb("cosw", [64, NT], F32)
    sinw = ph.sb("sinw", [64, NT], F32)
    k.dma("sp", cosw[:], cos_win, writes=["cosw"], sb="cosw")
    k.dma("sp", sinw[:], sin_win, writes=["cosw"], sb="cosw")
    sq = ph.sb("sq", [128, 4, 512], F32)
    rstd = ph.sb("rstd", [128, 512], F32)
    qnb = ph.sb("qnb", [128, 4, 512], BF16)
    qst = [ph.sb("qst%d" % i, [128, 512], BF16) for i in range(2)]
    qrs = [ph.sb("qrs%d" % i, [64, 512], BF16) for i in range(2)]
    t1 = ph.sb("t1", [64, 512], F32)
    t2 = ph.sb("t2", [64, 512], F32)
    for (p0, n) in [(p0, min(512, NT - p0)) for p0 in range(0, NT, 512)]:
        for c in range(4):
            for kc in range(KC):
                k.op("pe", lambda e: e.matmul(ps[2 + c][:, 0:n], lhsT=wcq[:, kc, c * 128:(c + 1) * 128], rhs=hT[:, kc, p0:p0 + n],
                                              start=(kc == 0), stop=(kc == KC - 1)), reads=["wcq", "hT"], writes=[PB[2 + c]], mark=(kc == KC - 1))
            k.op("act", lambda e: e.activation(out=sq[:, c, 0:n], in_=ps[2 + c][:, 0:n], func=AF.Square), reads=[PB[2 + c]], writes=["sq"])
        for c in range(4):
            k.op("pe", lambda e: e.matmul(ps[6][:, 0:n], lhsT=onesf[:, :], rhs=sq[:, c, 0:n], start=(c == 0), stop=(c == 3)),
                 reads=["onesf", "sq"], writes=[PB[6]], mark=(c == 3))
        rsqrt_inplace(rstd[:, 0:n], ps[6][:, 0:n], QR, [PB[6]], ["rstd"], epsc[:, :])
        for c in range(4):
            k.op("dve", lambda e: e.scalar_tensor_tensor(out=qnb[:, c, 0:n], in0=ps[2 + c][:, 0:n], scalar=gq[:, c:c + 1], in1=rstd[:, 0:n],
                                                         op0=ALU.mult, op1=ALU.mult), reads=[PB[2 + c], "gq", "rstd"], writes=["qnb"])
        for h in range(MH):
            si = h % 2
            o0 = h * (NOPE + ROPE)
            for c in range(4):
                k.op("pe", lambda e: e.matmul(ps[si][:, 0:n], lhsT=wuq[:, c, o0:o0 + NOPE], rhs=qnb[:, c, 0:n], start=(c == 0), stop=(c == 3)),
                     reads=["wuq", "qnb"], writes=[PB[si]], mark=(c == 3))
            k.op("act", lambda e: e.copy(out=qst[si][:, 0:n], in_=ps[si][:, 0:n]), reads=[PB[si]], writes=["qst%d" % si])
            k.dma("sp", qn_scr[h, :, p0:p0 + n], qst[si][:, 0:n], reads=["qst%d" % si], writes=["qn_scr"], sb="qst%d" % si)
            for c in range(4):
                k.op("pe", lambda e: e.matmul(ps[7][0:64, 0:n], lhsT=wuq[:, c, o0 + NOPE:o0 + NOPE + ROPE], rhs=qnb[:, c, 0:n], start=(c == 0), stop=(c == 3)),
                     reads=["wuq", "qnb"], writes=[PB[7]], mark=(c == 3))
            k.op("dve", lambda e: e.tensor_tensor(out=t1[:, 0:n], in0=ps[7][0:64, 0:n], in1=cosw[:, p0:p0 + n], op=ALU.mult),
                 reads=[PB[7], "cosw"], writes=["t1"])
            for c in range(4):
                k.op("pe", lambda e: e.matmul(ps[7][0:64, 0:n], lhsT=wuqr[:, c, h * ROPE:(h + 1) * ROPE], rhs=qnb[:, c, 0:n], start=(c == 0), stop=(c == 3)),
                     reads=["wuqr", "qnb"], writes=[PB[7]], mark=(c == 3))
            k.op("dve", lambda e: e.tensor_tensor(out=t2[:, 0:n], in0=ps[7][0:64, 0:n], in1=sinw[:, p0:p0 + n], op=ALU.mult),
                 reads=[PB[7], "cosw"], writes=["t2"])
            k.op("dve", lambda e: e.tensor_tensor(out=qrs[si][:, 0:n], in0=t1[:, 0:n], in1=t2[:, 0:n], op=ALU.add),
                 reads=["t1", "t2"], writes=["qrs%d" % si])
            k.dma("sp", qr_scr[h, :, p0:p0 + n], qrs[si][:, 0:n], reads=["qrs%d" % si], writes=["qr_scr"], sb="qrs%d" % si)
    ph.close()

    def bank_pieces(qa, qb):
        out = []
        for b in range(3):
            lo, hi = max(qa, 512 * b), min(qb, 512 * (b + 1))
            if lo < hi:
                out.append((b, lo, hi - lo))
        return out

    prm = [(128 * i_, 128, i_, i_, (0, WINP), None) for i_ in range(NPT)]
    prm.append((NPRE, 16, NPT, None, (0, WINP), None))
    for m in range(1, 9):
        prm.append((NPRE + 16 + 128 * (m - 1), 128, NPT + m, None, (16 + 128 * (m - 1), WINP), 16 + 128 * (m - 1)))
    prob_prompt = [((0, WINP), prm)]
    prob_sample = []
    for s_ in range(NSS):
        qa = WINP + TS * s_
        tl = [(NPRE + NT + PAST * s_ + 128 * j, 128, NPT + 9 + NSS + 8 * s_ + j, None, (qa, qa + TS), None) for j in range(8)]
        tl.append((NPRE + qa, TS, NPT + 9 + s_, None, (qa, qa + TS), None))
        prob_sample.append(((qa, qa + TS), tl))

    def attention_phase(problems, kbase, ncols, vbase, nvt, qoff, nq):
        ph = Phase()
        kpeT = ph.sb("kpeT", [128, ncols], BF16)
        k.op("pool", lambda e: e.memset(kpeT[64:128, :], 0.0), writes=["kpeT"])
        k.dma("sp", kpeT[0:64, :], kpe_scr[:, kbase:kbase + ncols], writes=["kpeT"], sb="kpeT")
        kTh = [ph.sb("kTh%d" % i, [128, ncols], BF16) for i in range(2)]
        vh = [ph.sb("vh%d" % i, [128, nvt, MV], BF16) for i in range(2)]
        qnh = [ph.sb("qnh%d" % i, [128, nq], BF16) for i in range(2)]
        qrh = [ph.sb("qrh%d" % i, [128, nq], BF16) for i in range(2)]
        wgb = [ph.sb("wgb%d" % i, [128, KC, 128], BF16) for i in range(2)]
        sgb = [ph.sb("sgb%d" % i, [128, nq], F32) for i in range(2)]
        for i in range(2):
            k.op("pool", lambda e: e.memset(qrh[i][64:128, :], 0.0), writes=["qrh%d" % i])
        pTs = [ph.sb("pT%d" % i, [128, 512], BF16) for i in range(2)]
        pbias = ph.sb("pbias", [128, NPT], F32)
        k.dma("sp", pbias[:], pre_bias, writes=["pbias"], sb="pbias")
        rl = ph.sb("rl", [128, 512], F32)
        ot = ph.sb("ot", [128, 512], F32)
        st_i = [0]

        def load_head(h):
            i = h % 2
            k.dma("sp", kTh[i][:], kT_scr[h, :, kbase:kbase + ncols], writes=["kTh%d" % i], sb="kTh%d" % i)
            k.dma("sp", vh[i][:], v_scr[vbase:vbase + nvt, :, h * MV:(h + 1) * MV].rearrange("t p c -> p t c"), writes=["vh%d" % i], sb="vh%d" % i)
            k.dma("sp", qnh[i][:], qn_scr[h, :, qoff:qoff + nq], writes=["qnh%d" % i], sb="qnh%d" % i)
            k.dma("sp", qrh[i][0:64, :], qr_scr[h, :, qoff:qoff + nq], writes=["qrh%d" % i], sb="qrh%d" % i)
            wslab(wgb[i][:], w_in[:, O_GB + h * MV:O_GB + (h + 1) * MV], "wgb%d" % i)

        if MH_RUN:
            load_head(0)
        for h in range(MH_RUN):
            i = h % 2
            KT, KTN, VH, VHN = kTh[i], "kTh%d" % i, vh[i], "vh%d" % i
            QN, QNN, QR, QRN = qnh[i], "qnh%d" % i, qrh[i], "qrh%d" % i
            SG, SGN = sgb[i], "sgb%d" % i
            if h + 1 < MH_RUN:
                load_head(h + 1)
            for p0 in range(0, nq, 512):
                n = min(512, nq - p0)
                b = 6 + st_i[0] % 2
                st_i[0] += 1
                for kc in range(KC):
                    k.op("pe", lambda e: e.matmul(ps[b][:, 0:n], lhsT=wgb[i][:, kc, :], rhs=hT[:, kc, qoff + p0:qoff + p0 + n], start=(kc == 0), stop=(kc == KC - 1)),
                         reads=["wgb%d" % i, "hT"], writes=[PB[b]], mark=(kc == KC - 1))
                k.op("act", lambda e: e.activation(out=SG[:, p0:p0 + n], in_=ps[b][:, 0:n], func=AF.Sigmoid), reads=[PB[b]], writes=[SGN])
            for (qrange, tl) in problems:
                bps = bank_pieces(*qrange)
                first = {b: None for (b, _, _) in bps}
                last = {}
                for ti, (kcol, nk, vt, bi, (ra, rb), fix) in enumerate(tl):
                    for (b, lo, n) in bank_pieces(ra, rb):
                        if first[b] is None:
                            first[b] = ti
                        last[b] = ti
                items = [(ti, kcol - kbase, nk, vt - vbase, bi, fix, b, lo, n) for ti, (kcol, nk, vt, bi, (ra, rb), fix) in enumerate(tl)
                         for (b, lo, n) in bank_pieces(ra, rb)]

                def qk_exp(it, slot):
                    ti, kcol, nk, vt, bi, fix, b, lo, n = it
                    sbk = 6 + slot
                    PT, PTN = pTs[slot], "pT%d" % slot
                    k.op("pe", lambda e: e.matmul(ps[sbk][0:nk, 0:n], lhsT=KT[:, kcol:kcol + nk], rhs=QN[:, lo - qoff:lo - qoff + n], start=True, stop=False),
                         reads=[KTN, QNN], writes=[PB[sbk]], mark=False)
                    k.op("pe", lambda e: e.matmul(ps[sbk][0:nk, 0:n], lhsT=kpeT[:, kcol:kcol + nk], rhs=QR[:, lo - qoff:lo - qoff + n], start=False, stop=True),
                         reads=["kpeT", QRN], writes=[PB[sbk]])
                    if bi is not None:
                        k.op("act", lambda e: e.activation(out=PT[0:nk, 0:n], in_=ps[sbk][0:nk, 0:n], func=AF.Exp, scale=SCALE, bias=pbias[0:nk, bi:bi + 1]),
                             reads=[PB[sbk], "pbias"], writes=[PTN])
                    else:
                        k.op("act", lambda e: e.activation(out=PT[0:nk, 0:n], in_=ps[sbk][0:nk, 0:n], func=AF.Exp, scale=SCALE),
                             reads=[PB[sbk]], writes=[PTN])
                    if fix is not None and lo <= fix < lo + n:
                        f0 = fix - lo
                        k.op("pool", lambda e: e.memset(PT[64:128, f0:f0 + 64], 0.0), reads=[], writes=[PTN])

                def pv_sum(it, slot):
                    ti, kcol, nk, vt, bi, fix, b, lo, n = it
                    PT, PTN = pTs[slot], "pT%d" % slot
                    c0 = lo - 512 * b
                    k.op("pe", lambda e: e.matmul(ps[b][:, c0:c0 + n], lhsT=VH[0:nk, vt, :], rhs=PT[0:nk, 0:n],
                                                  start=(ti == first[b]), stop=(ti == last[b])), reads=[VHN, PTN], writes=[PB[b]], mark=False)
                    k.op("pe", lambda e: e.matmul(ps[3 + b][:, c0:c0 + n], lhsT=onesb[0:nk, :], rhs=PT[0:nk, 0:n],
                                                  start=(ti == first[b]), stop=(ti == last[b])), reads=["onesb", PTN], writes=[PB[3 + b]])

                for idx in range(len(items) + 1):
                    if idx < len(items):
                        qk_exp(items[idx], idx % 2)
                    if idx >= 1:
                        pv_sum(items[idx - 1], (idx - 1) % 2)
                for (b, lo, n) in bps:
                    c0 = lo - 512 * b
                    k.op("dve", lambda e: e.reciprocal(out=rl[:, 0:n], in_=ps[3 + b][:, c0:c0 + n]), reads=[PB[3 + b]], writes=["rl"])
                    k.op("dve", lambda e: e.tensor_tensor(out=ot[:, 0:n], in0=ps[b][:, c0:c0 + n], in1=rl[:, 0:n], op=ALU.mult),
                         reads=[PB[b], "rl"], writes=["ot"])
                    k.op("dve", lambda e: e.tensor_tensor(out=ot[:, 0:n], in0=ot[:, 0:n], in1=SG[:, lo - qoff:lo - qoff + n], op=ALU.mult),
                         reads=["ot", SGN], writes=["ot"])
                    k.op("dve", lambda e: e.tensor_tensor(out=mergedT[:, h, lo:lo + n], in0=mergedT[:, h, lo:lo + n], in1=ot[:, 0:n], op=ALU.add),
                         reads=["ot", "mergedT"], writes=["mergedT"])
        ph.close()

    attention_phase(prob_prompt, 0, NPRE + WINP, 0, NPT + 9, 0, WINP)
    attention_phase(prob_sample, NPRE + WINP, NSS * TS + NSS * PAST, NPT + 9, NSS + NSS * 8, WINP, NSS * TS)

    pieces = [(p0, min(512, NT - p0)) for p0 in range(0, NT, 512)]
    rstd2 = sb("rstd2", [128, NT], F32)
    ph = Phase()
    wsl = [ph.sb("wsl%d" % i, [128, KC, 128], BF16) for i in range(2)]
    xTc = [ph.sb("xTc%d" % i, [128, NT], F32) for i in range(2)]
    xnb = [ph.sb("xnb%d" % i, [128, NT], F32) for i in range(2)]
    sq2 = ph.sb("sq2", [128, NT], F32)
    gffn = ph.sb("gffn", [128, KC], F32)
    k.dma("sp", gffn[:], g_ffn_pk, writes=["gffn"], sb="gffn")
    for c in range(KC):
        W, WN = wsl[c % 2], "wsl%d" % (c % 2)
        XC, XCN = xTc[c % 2], "xTc%d" % (c % 2)
        XN_, XNN = xnb[c % 2], "xnb%d" % (c % 2)
        wslab(W[:], w_o[:, c * 128:(c + 1) * 128], WN)
        k.dma("sp", XC[:], xT_scr[c], reads=["xT_scr"], writes=[XCN], sb=XCN)
        for pi, (p0, n) in enumerate(pieces):
            for kc in range(KC):
                k.op("pe", lambda e: e.matmul(ps[pi][:, 0:n], lhsT=W[:, kc, :], rhs=mergedT[:, kc, p0:p0 + n], start=(kc == 0), stop=(kc == KC - 1)),
                     reads=[WN, "mergedT"], writes=[PB[pi]], mark=(kc == KC - 1))
            k.op("dve", lambda e: e.tensor_tensor(out=XN_[:, p0:p0 + n], in0=ps[pi][:, 0:n], in1=XC[:, p0:p0 + n], op=ALU.add),
                 reads=[PB[pi], XCN], writes=[XNN])
        k.op("act", lambda e: e.activation(out=sq2[:, :], in_=XN_[:, :], func=AF.Square), reads=[XNN], writes=["sq2"])
        for pi, (p0, n) in enumerate(pieces):
            k.op("pe", lambda e: e.matmul(ps[5 + pi][:, 0:n], lhsT=onesf[:, :], rhs=sq2[:, p0:p0 + n], start=(c == 0), stop=(c == KC - 1)),
                 reads=["onesf", "sq2"], writes=[PB[5 + pi]])
        k.op("dve", lambda e: e.tensor_scalar(out=hT[:, c, :], in0=XN_[:, :], scalar1=gffn[:, c:c + 1], scalar2=None, op0=ALU.mult),
             reads=[XNN, "gffn"], writes=["hT"])
        k.dma("sp", xn_scr[c], XN_[:, :], reads=[XNN], writes=["xn_scr"], sb=XNN)
    for pi, (p0, n) in enumerate(pieces):
        rsqrt_inplace(rstd2[:, p0:p0 + n], ps[5 + pi][:, 0:n], D, [PB[5 + pi]], ["rstd2"], epsc[:, :])
    for c in range(KC):
        k.op("dve", lambda e: e.tensor_tensor(out=hT[:, c, :], in0=hT[:, c, :], in1=rstd2[:, :], op=ALU.mult),
             reads=["hT", "rstd2"], writes=["hT"])
    ph.close()

    UE = WINP + NSS * (TS + 2)
    phg = Phase()
    gT2 = phg.sb("gT2", [128, GC - KC, NT], BF16)
    ph = Phase()

    def gTj(j):
        return (mergedT[:, j, :], "mergedT") if j < KC else (gT2[:, j - KC, :], "gT2")

    k.op("pool", lambda e: e.memset(mergedT[:], 0.0), writes=["mergedT"])
    k.op("pool", lambda e: e.memset(gT2[:], 0.0), writes=["gT2"])
    wup = [ph.sb("wup%d" % i, [128, KC, 256], BF16) for i in range(4)]
    ue = [ph.sb("ue%d" % i, [128, UE], F32) for i in range(2)]
    cc = [ph.sb("cc%d" % i, [128, UE], F32) for i in range(2)]
    histT = ph.sb("histT", [128, FC, 2 * NSS], F32)
    hrow = ph.sb("hrow", [2 * NSS, 512], F32)
    cwt = ph.sb("cwt", [128, 3, FC], F32)
    cbt = ph.sb("cbt", [128, FC], F32)
    cst = ph.sb("cst", [128, FC, 2 + 2 * NSS], F32)
    cvs = ph.sb("cvs", [2 + 2 * NSS, 512], F32)
    k.dma("sp", cwt[:], conv_w_pk, writes=["cwt"], sb="cwt")
    k.dma("sp", cbt[:], conv_b_pk, writes=["cbt"], sb="cbt")
    for g4 in range(FC // 4):
        k.dma("sp", hrow[:], c_conv[:, :, g4 * 512:(g4 + 1) * 512].rearrange("s t c -> (s t) c"), writes=["hrow"], sb="hrow")
        for j in range(4):
            k.op("pe", lambda e: e.transpose(out=ps[6][:, j * 8:(j + 1) * 8], in_=hrow[:, j * 128:(j + 1) * 128], identity=identf[0:8, 0:8]),
                 reads=["hrow", "identf"], writes=[PB[6]], mark=(j == 3))
        k.op("dve", lambda e: e.tensor_copy(out=histT[:, g4 * 4:(g4 + 1) * 4, :], in_=ps[6][:, 0:32].rearrange("p (j t) -> p j t", j=4)),
             reads=[PB[6]], writes=["histT"])
    samp = lambda ap, w, a, b: ap.rearrange("p (s t) -> p s t", t=w)[:, :, a:b]
    for j in range(GC):
        for half in range(2):
            fc = j + half * GC
            wi = (2 * (j // 2) + half) % 4
            W, WN = wup[wi], "wup%d" % wi
            U, UN = ue[half], "ue%d" % half
            C_, CN = cc[half], "cc%d" % half
            wc0 = (j % 2) * 128
            if j % 2 == 0:
                wslab(W[:], w_up[:, fc * 128:(fc + 2) * 128], WN)
            for pi, (p0, n) in enumerate(pieces):
                b = half * 3 + pi
                for kc in range(KC):
                    k.op("pe", lambda e: e.matmul(ps[b][:, 0:n], lhsT=W[:, kc, wc0:wc0 + 128], rhs=hT[:, kc, p0:p0 + n], start=(kc == 0), stop=(kc == KC - 1)),
                         reads=[WN, "hT"], writes=[PB[b]], mark=(kc == KC - 1))
                if pi < 2:
                    k.op("act", lambda e: e.copy(out=U[:, p0:p0 + n], in_=ps[b][:, 0:n]), reads=[PB[b]], writes=[UN])
                else:
                    k.op("act", lambda e: e.copy(out=U[:, 1024:WINP], in_=ps[b][:, 0:16]), reads=[PB[b]], writes=[UN])
                    k.op("act", lambda e: e.copy(out=samp(U[:, WINP:UE], TS + 2, 2, TS + 2), in_=samp(ps[b][:, 16:16 + NSS * TS], TS, 0, TS)),
                         reads=[PB[b]], writes=[UN])
            k.op("pool", lambda e: e.tensor_copy(out=samp(U[:, WINP:UE], TS + 2, 0, 2), in_=histT[:, fc, :].rearrange("p (s t) -> p s t", t=2)),
                 reads=["histT"], writes=[UN])
            k.op("pool", lambda e: e.tensor_copy(out=cst[:, fc, 0:2], in_=U[:, WINP - 2:WINP]), reads=[UN], writes=["cst"])
            k.op("pool", lambda e: e.tensor_copy(out=cst[:, fc, 2:2 + 2 * NSS].rearrange("p (s t) -> p s t", t=2), in_=samp(U[:, WINP:UE], TS + 2, TS, TS + 2)),
                 reads=[UN], writes=["cst"])
            k.op("dve", lambda e: e.tensor_scalar(out=C_[:, 0:UE - 2], in0=U[:, 2:UE], scalar1=cwt[:, 2, fc:fc + 1], scalar2=cbt[:, fc:fc + 1],
                                                  op0=ALU.mult, op1=ALU.add), reads=[UN, "cwt", "cbt"], writes=[CN])
            k.op("dve", lambda e: e.scalar_tensor_tensor(out=C_[:, 0:UE - 2], in0=U[:, 1:UE - 1], scalar=cwt[:, 1, fc:fc + 1], in1=C_[:, 0:UE - 2],
                                                         op0=ALU.mult, op1=ALU.add), reads=[UN, "cwt", CN], writes=[CN])
            k.op("dve", lambda e: e.scalar_tensor_tensor(out=C_[:, 0:UE - 2], in0=U[:, 0:UE - 2], scalar=cwt[:, 0, fc:fc + 1], in1=C_[:, 0:UE - 2],
                                                         op0=ALU.mult, op1=ALU.add), reads=[UN, "cwt", CN], writes=[CN])
        k.op("act", lambda e: e.activation(out=cc[0][:, 0:UE - 2], in_=cc[0][:, 0:UE - 2], func=AF.Silu), reads=["cc0"], writes=["cc0"])
        gt, gn = gTj(j)
        k.op("dve", lambda e: e.tensor_tensor(out=gt[:, 2:WINP], in0=cc[0][:, 0:WINP - 2], in1=cc[1][:, 0:WINP - 2], op=ALU.mult),
             reads=["cc0", "cc1"], writes=[gn])
        k.op("dve", lambda e: e.tensor_tensor(out=samp(gt[:, WINP:NT], TS, 0, TS), in0=samp(cc[0][:, WINP:UE], TS + 2, 0, TS),
                                              in1=samp(cc[1][:, WINP:UE], TS + 2, 0, TS), op=ALU.mult), reads=["cc0", "cc1"], writes=[gn])
    for g4 in range(FC // 4):
        for j in range(4):
            k.op("pe", lambda e: e.transpose(out=ps[7][0:2 + 2 * NSS, j * 128:(j + 1) * 128], in_=cst[:, g4 * 4 + j, :], identity=identf[:, :]),
                 reads=["cst", "identf"], writes=[PB[7]], mark=(j == 3))
        k.op("dve", lambda e: e.tensor_copy(out=cvs[:, :], in_=ps[7][0:2 + 2 * NSS, :]), reads=[PB[7]], writes=["cvs"])
        k.dma("sp", o_conv[:, g4 * 512:(g4 + 1) * 512], cvs[:, :], reads=["cvs"], writes=["o_conv"], sb="cvs")

    ph.close()
    ph = Phase()
    wdn = [ph.sb("wdn%d" % i, [128, GC, 128], BF16) for i in range(2)]
    xnc = [ph.sb("xnc%d" % i, [128, NT], F32) for i in range(2)]
    xos = [ph.sb("xos%d" % i, [128, NT], F32) for i in range(2)]
    for c in range(KC):
        W, WN = wdn[c % 2], "wdn%d" % (c % 2)
        XC, XCN = xnc[c % 2], "xnc%d" % (c % 2)
        XO, XON = xos[c % 2], "xos%d" % (c % 2)
        wslab(W[:], w_down[:, c * 128:(c + 1) * 128], WN)
        k.dma("sp", XC[:], xn_scr[c], reads=["xn_scr"], writes=[XCN], sb=XCN)
        for pi, (p0, n) in enumerate(pieces):
            for j in range(GC):
                gt, gn = gTj(j)
                k.op("pe", lambda e: e.matmul(ps[pi][:, 0:n], lhsT=W[:, j, :], rhs=gt[:, p0:p0 + n], start=(j == 0), stop=(j == GC - 1)),
                     reads=[WN, gn], writes=[PB[pi]], mark=(j == GC - 1))
            k.op("dve", lambda e: e.tensor_tensor(out=XO[:, p0:p0 + n], in0=ps[pi][:, 0:n], in1=XC[:, p0:p0 + n], op=ALU.add),
                 reads=[PB[pi], XCN], writes=[XON])
        k.dma("sp", xo_scr[c], XO[:, :], reads=[XON], writes=["xo_scr"], sb=XON)
    ph.close()
    phg.close()

    ph = Phase()
    finb = ph.sb("finb", [128, D], F32)
    k.dma("sp", finb[:], fin_bc, writes=["finb"], sb="finb")
    xo_t = [ph.sb("xo_t%d" % i, [128, KC, 128], F32) for i in range(2)]
    ysb = [ph.sb("ysb%d" % i, [128, D], F32) for i in range(2)]
    junk = ph.sb("junk", [128, 512], F32)
    ssy = ph.sb("ssy", [128, 8], F32)
    for ti, (t0, n) in enumerate([(t0, min(128, NT - t0)) for t0 in range(0, NT, 128)]):
        XT, XTN = xo_t[ti % 2], "xo_t%d" % (ti % 2)
        Y, YN = ysb[ti % 2], "ysb%d" % (ti % 2)
        k.dma("sp", XT[:, :, 0:n], xo_scr[:, :, t0:t0 + n].rearrange("kc p t -> p kc t"), reads=["xo_scr"], writes=[XTN], sb=XTN)
        for g in range(4):
            b = (ti % 2) * 4 + g
            for j in range(4):
                kc = g * 4 + j
                k.op("pe", lambda e: e.transpose(out=ps[b][0:n, j * 128:(j + 1) * 128], in_=XT[:, kc, 0:n], identity=identf[:, :]),
                     reads=[XTN, "identf"], writes=[PB[b]], mark=(j == 3))
            k.op("act", lambda e: e.activation(out=junk[0:n, :], in_=ps[b][0:n, :], func=AF.Square, accum_out=ssy[0:n, g:g + 1]),
                 reads=[PB[b]], writes=["junk", "ssy"])
        k.op("dve", lambda e: e.tensor_tensor(out=ssy[0:n, 4:5], in0=ssy[0:n, 0:1], in1=ssy[0:n, 1:2], op=ALU.add), reads=["ssy"], writes=["ssy"])
        k.op("dve", lambda e: e.tensor_tensor(out=ssy[0:n, 5:6], in0=ssy[0:n, 2:3], in1=ssy[0:n, 3:4], op=ALU.add), reads=["ssy"], writes=["ssy"])
        k.op("dve", lambda e: e.tensor_tensor(out=ssy[0:n, 6:7], in0=ssy[0:n, 4:5], in1=ssy[0:n, 5:6], op=ALU.add), reads=["ssy"], writes=["ssy"])
        rsqrt_inplace(ssy[0:n, 7:8], ssy[0:n, 6:7], D, ["ssy"], ["ssy"], epsc[0:n, :])
        for g in range(4):
            b = (ti % 2) * 4 + g
            k.op("dve", lambda e: e.scalar_tensor_tensor(out=Y[0:n, g * 512:(g + 1) * 512], in0=ps[b][0:n, :], scalar=ssy[0:n, 7:8],
                                                         in1=finb[0:n, g * 512:(g + 1) * 512], op0=ALU.mult, op1=ALU.mult),
                 reads=[PB[b], "ssy", "finb"], writes=[YN])
        k.dma("sp", o_y[t0:t0 + n, :], Y[0:n, :], reads=[YN], writes=["o_y"], sb=YN)
    ph.close()

    if K_DBG:
        o_dbg = dout("o_dbg", [128, KC, NT], BF16)
        k.dma("sp", o_dbg, mergedT[:], reads=["mergedT"], writes=["o_dbg"], sb="mergedT")
    k.barrier()
    return nc


_CACHE = {}


def _rope_tables(pos):
    inv = (10000.0 ** (-np.arange(0, ROPE, 2, dtype=np.float32) / ROPE)).astype(np.float32)
    ang = pos.astype(np.float32)[:, None] * inv[None, :]
    c, s = np.cos(ang).astype(np.float32), np.sin(ang).astype(np.float32)
    return (np.ascontiguousarray(np.concatenate([c, c], 1).T),
            np.ascontiguousarray(np.concatenate([-s, s], 1).T))


def kernel(x_prompt, x_sample, cache_mla_latent, cache_mla_krope, state_gla, cache_ffn_conv, meta_tokens,
           g_mix, w_in, w_a2, b_a, g_gla_out, g_q, w_uq, g_kv, w_uk, w_uv, w_o, g_ffn, w_up, conv_w, conv_b,
           w_down, final_norm):
    f = lambda a: np.ascontiguousarray(np.asarray(a, dtype=np.float32))
    x_prompt, x_sample, meta_tokens, w_in = f(x_prompt), f(x_sample), f(meta_tokens), f(w_in)
    if "nc" not in _CACHE:
        _CACHE["nc"] = build_program()
    nc = _CACHE["nc"]
    ext = np.concatenate([meta_tokens, x_prompt[0]], 0)
    pk = lambda v, n: np.ascontiguousarray(f(v).reshape(n, 128).T)
    bc = lambda v: np.ascontiguousarray(np.broadcast_to(f(v).reshape(1, -1), (128, f(v).size)))
    perm = np.concatenate([np.arange(32, 64), np.arange(0, 32)])
    cos_p, sin_p = _rope_tables(np.arange(NPRE))
    shared = {
        "x_pre": np.ascontiguousarray(ext[:NPRE]),
        "cos_pre": cos_p, "sin_pre": sin_p,
        "w_in": w_in[0], "w_kpe_rot": np.ascontiguousarray(w_in[0][:, O_KPE + perm]),
        "w_a2": f(w_a2)[0], "b_a_bc": bc(f(b_a)[0]), "g_mix_bc": bc(f(g_mix)[0]),
        "g_gla_pk": pk(f(g_gla_out)[0], 4), "g_q_pk": pk(f(g_q)[0], 4), "g_kv_pk": pk(f(g_kv)[0], 4),
        "g_ffn_pk": pk(f(g_ffn)[0], KC), "fin_bc": bc(final_norm),
        "w_uq": f(w_uq)[0],
        "w_uq_rot": np.ascontiguousarray(f(w_uq)[0].reshape(QR, MH, NOPE + ROPE)[:, :, NOPE + perm].reshape(QR, MH * ROPE)),
        "w_uk": f(w_uk)[0], "w_uv": f(w_uv)[0], "w_o": f(w_o)[0], "w_up": f(w_up)[0],
        "conv_w_pk": np.ascontiguousarray(f(conv_w)[0].reshape(3, FC, 128).transpose(2, 0, 1)),
        "conv_b_pk": pk(f(conv_b)[0], FC), "w_down": f(w_down)[0],
        "c_ident": np.eye(128, dtype=np.float32), "c_ls": np.tril(np.ones((128, 128), np.float32), -1), "c_ui": np.triu(np.ones((128, 128), np.float32)), "c_ones": np.ones((128, 128), np.float32),
    }
    in_maps = []
    for c in range(NCORES):
        pos = np.concatenate([np.arange(1024 * c, 1024 * c + WINP)] + [PAST + np.arange(TS)] * NSS)
        cw, sw = _rope_tables(pos)
        m = dict(shared)
        m["x_win"] = np.ascontiguousarray(np.concatenate(
            [ext[1024 * c:1024 * c + WINP], x_sample[NSS * c:NSS * c + NSS].reshape(NSS * TS, D)], 0))
        valid = (np.arange(NPT)[None, :] < 8 * c).astype(np.float32) * np.ones((128, 1), np.float32)
        m["pre_valid"] = np.ascontiguousarray(valid)
        m["pre_bias"] = np.ascontiguousarray((1.0 - valid) * NEG)
        m["cos_win"], m["sin_win"] = cw, sw
        m["c_lat"] = f(cache_mla_latent)[0, NSS * c:NSS * c + NSS]
        m["c_kr"] = f(cache_mla_krope)[0, NSS * c:NSS * c + NSS]
        m["c_gla"] = f(state_gla)[0, NSS * c:NSS * c + NSS]
        m["c_conv"] = f(cache_ffn_conv)[0, NSS * c:NSS * c + NSS]
        in_maps.append(m)
    used = _CACHE.get("used")
    if used is None:
        used = _CACHE["used"] = set(_input_names(nc))
    in_maps = [{kk: vv for kk, vv in m.items() if kk in used} for m in in_maps]
    res = run_bass_kernel_spmd(nc, in_maps, core_ids=list(range(NCORES))).results

    y_p = np.zeros((1, SEQ, D), np.float32)
    y_s = np.zeros((32, TS, D), np.float32)
    lat_p = np.zeros((1, 1, EXT, KVR), np.float32)
    kr_p = np.zeros((1, 1, EXT, ROPE), np.float32)
    gla_p = np.zeros((1, 1, GH, GDK, GDV), np.float32)
    conv_p = np.zeros((1, 1, 2, 2 * DFF), np.float32)
    lat_s = np.zeros((1, 32, TS, KVR), np.float32)
    kr_s = np.zeros((1, 32, TS, ROPE), np.float32)
    gla_s = np.zeros((1, 32, GH, GDK, GDV), np.float32)
    conv_s = np.zeros((1, 32, 2, 2 * DFF), np.float32)
    for c in range(NCORES):
        r = res[c]
        lo = 0 if c == 0 else 16
        lat_p[0, 0, 1024 * c + lo:1024 * c + WINP] = r["o_lat"][lo:WINP]
        kr_p[0, 0, 1024 * c + lo:1024 * c + WINP] = r["o_kr"][lo:WINP]
        lat_s[0, NSS * c:NSS * c + NSS] = r["o_lat"][WINP:].reshape(NSS, TS, KVR)
        kr_s[0, NSS * c:NSS * c + NSS] = r["o_kr"][WINP:].reshape(NSS, TS, ROPE)
        if c == NCORES - 1:
            gla_p[0, 0] = r["o_gla_p"]
        gla_s[0, NSS * c:NSS * c + NSS] = r["o_gla_s"]
        if c == NCORES - 1:
            conv_p[0, 0] = r["o_conv"][0:2]
        conv_s[0, NSS * c:NSS * c + NSS] = r["o_conv"][2:].reshape(NSS, 2, 2 * DFF)
        if "o_dbg" in r and c == 0:
            _CACHE["dbg"] = np.asarray(r["o_dbg"]).astype(np.float32)
        if "o_y" in r:
            y_p[0, 1024 * c:1024 * c + 1024] = r["o_y"][16:WINP]
            y_s[NSS * c:NSS * c + NSS] = r["o_y"][WINP:].reshape(NSS, TS, D)
    return (y_p, y_s, lat_p, kr_p, gla_p, conv_p, lat_s, kr_s, gla_s, conv_s)


def _input_names(nc):
    return list(IN_NAMES)
```

```python
from contextlib import ExitStack
import numpy as np
import ml_dtypes
import concourse.bass as bass
import concourse.mybir as mybir
from concourse.bass_utils import run_bass_kernel_spmd

F32 = mybir.dt.float32
BF16 = mybir.dt.bfloat16
AF = mybir.ActivationFunctionType
ALU = mybir.AluOpType

NCORES = 8
D = 2048
KC = 16
SEQ = 8192
NMETA = 16
EXT = SEQ + NMETA
NPRE = 7168
NPT = NPRE // 128
WINP = 1040
NSS = 4
TS = 16
NT = WINP + NSS * TS
PAST = 1024
EPS = 1e-6
GH, GDK, GDV = 4, 256, 512
MH, NOPE, ROPE, MV = 16, 128, 64, 128
QR, KVR = 512, 512
DFF = 5632
FC = 2 * DFF // 128
GC = DFF // 128
O_Q, O_K, O_V, O_R, O_A = 0, 1024, 2048, 4096, 6144
O_CQ, O_CKV, O_KPE, O_GA, O_GB = 6160, 6672, 7184, 7248, 9296
INC = 11344
SCALE = (NOPE + ROPE) ** -0.5
NEG = -30000.0
import os
NPT_RUN = NPT
NPRE_BLK = int(os.environ.get('K_NPRE_BLK', NPRE // 512))
NSS_RUN = int(os.environ.get('K_NSS_RUN', NSS))
K_P2 = int(os.environ.get('K_P2', 2))
GH_RUN = int(os.environ.get('K_GH', GH))
K_DBG = int(os.environ.get('K_DBG', 0))
MH_RUN = int(os.environ.get('K_MH', MH))
NPT_RUN = int(os.environ.get('K_NPT', NPT))
K_3B = int(os.environ.get('K_3B', 127))


class Tok:
    __slots__ = ("sem", "val", "eng", "sid")

    def __init__(self, sem, val, eng, sid):
        self.sem, self.val, self.eng, self.sid = sem, val, eng, sid


class Buf:
    def __init__(self, name):
        self.name = name
        self.w = None
        self.r = []
        self.dsem = None
        self.dcnt = 0


class K:
    def __init__(self, nc):
        self.nc = nc
        self.engs = {"pe": nc.tensor, "act": nc.scalar, "dve": nc.vector, "pool": nc.gpsimd, "sp": nc.sync}
        self.sem = {}
        self.cnt = {}
        self.waited = {e: {} for e in self.engs}
        self.nsem = 0
        for e in ("pe", "act", "dve", "pool"):
            self.sem[e] = nc.alloc_semaphore("prog_" + e)
            self.cnt[e] = 0
        self.pe_pending = []
        self.bufs = {}
        self.dsems = {}

    def buf(self, name):
        b = self.bufs.get(name)
        if b is None:
            b = Buf(name)
            self.bufs[name] = b
        return b

    def _wait(self, e, tok):
        if tok is None:
            return
        if tok.eng == "pe" and e == "pe":
            return
        w = self.waited[e]
        if w.get(tok.sid, 0) >= tok.val:
            return
        self.engs[e].wait_ge(tok.sem, tok.val)
        w[tok.sid] = tok.val

    def _deps(self, e, reads, writes):
        need = {}

        def want(tok):
            if tok is None:
                return
            cur = need.get(tok.sid)
            if cur is None or tok.val > cur.val:
                need[tok.sid] = tok

        for b in reads:
            want(b.w)
        for b in writes:
            if b in self.pe_pending and e != "pe":
                raise RuntimeError("write to buffer with unmarked PE reads: " + b.name)
            want(b.w)
            for r in b.r:
                if r.eng == e and e not in ("sp",):
                    continue
                want(r)
        for tok in need.values():
            self._wait(e, tok)

    def _commit(self, tok, reads, writes):
        for b in reads:
            b.r.append(tok)
        for b in writes:
            b.w = tok
            b.r = []

    def op(self, e, fn, reads=(), writes=(), mark=True):
        reads = [self.buf(b) if isinstance(b, str) else b for b in reads]
        writes = [self.buf(b) if isinstance(b, str) else b for b in writes]
        self._deps(e, reads, writes)
        ins = fn(self.engs[e])
        if not mark:
            assert e == "pe"
            for b in reads:
                if b not in self.pe_pending:
                    self.pe_pending.append(b)
            return None
        self.cnt[e] += 1
        ins.then_inc(self.sem[e], 1)
        tok = Tok(self.sem[e], self.cnt[e], e, "prog_" + e)
        if e == "pe" and self.pe_pending:
            for b in self.pe_pending:
                b.r.append(tok)
            self.pe_pending = []
        self._commit(tok, reads, writes)
        return tok

    def dma(self, q, out, in_, reads=(), writes=(), sb=None):
        reads = [self.buf(b) for b in reads if b not in DRAM_NAMES]
        writes = [self.buf(b) for b in writes if b not in DRAM_NAMES]
        sb = self.buf(sb) if isinstance(sb, str) else sb
        ds = self.dsems.get(sb.name)
        if ds is None:
            ds = self.dsems[sb.name] = [self.nc.alloc_semaphore("d_" + sb.name), 0]
            self.nsem += 1
        self._deps(q, reads, writes)
        ins = self.engs[q].dma_start(out=out, in_=in_)
        ds[1] += 16
        ins.then_inc(ds[0], 16)
        tok = Tok(ds[0], ds[1], "dma", "d_" + sb.name)
        self._commit(tok, reads, writes)
        return tok

    def barrier(self):
        assert not self.pe_pending
        toks = [Tok(self.sem[e], self.cnt[e], e, "prog_" + e) for e in self.sem if self.cnt[e] > 0]
        toks += [Tok(d[0], d[1], "dma", "d_" + n) for n, d in self.dsems.items() if d[1] > 0]
        for e in self.engs:
            for t in toks:
                if t.eng == e:
                    continue
                self._wait(e, t)
        self.bufs = {}

    def wait_all(self, e, bufs):
        for b in bufs:
            b = self.buf(b) if isinstance(b, str) else b
            self._wait(e, b.w)
            for r in b.r:
                self._wait(e, r)


IN_NAMES = []
DRAM_NAMES = {"o_y", "o_lat", "o_kr", "o_gla_p", "o_gla_s", "o_conv", "o_dbg", "kT_scr", "kpe_scr", "v_scr", "qn_scr", "qr_scr",
              "xT_scr", "xn_scr", "xo_scr", "Spre_scr", "hpre_scr"}


def _bf(a):
    return np.ascontiguousarray(a).astype(ml_dtypes.bfloat16)


def build_program():
    nc = bass.Bass("TRN2", target_bir_lowering=False)
    k = K(nc)

    def din(name, shape, dt=F32):
        IN_NAMES.append(name)
        return nc.dram_tensor(name, list(shape), dt, kind="ExternalInput").ap()

    def dout(name, shape, dt=F32):
        return nc.dram_tensor(name, list(shape), dt, kind="ExternalOutput").ap()

    def dscr(name, shape, dt):
        return nc.dram_tensor(name, list(shape), dt).ap()

    def sb(name, shape, dt):
        return nc.alloc_sbuf_tensor(name, list(shape), dt)

    x_pre = din("x_pre", [NPRE, D])
    x_win = din("x_win", [NT, D])
    pre_valid = din("pre_valid", [128, NPT])
    pre_bias = din("pre_bias", [128, NPT])
    cos_pre = din("cos_pre", [64, NPRE])
    sin_pre = din("sin_pre", [64, NPRE])
    cos_win = din("cos_win", [64, NT])
    sin_win = din("sin_win", [64, NT])
    w_in = din("w_in", [D, INC])
    w_kpe_rot = din("w_kpe_rot", [D, ROPE])
    w_a2 = din("w_a2", [16, 1024])
    b_a_bc = din("b_a_bc", [128, 1024])
    g_mix_bc = din("g_mix_bc", [128, D])
    g_gla_pk = din("g_gla_pk", [128, 4])
    g_q_pk = din("g_q_pk", [128, 4])
    g_kv_pk = din("g_kv_pk", [128, 4])
    g_ffn_pk = din("g_ffn_pk", [128, KC])
    fin_bc = din("fin_bc", [128, D])
    w_uq = din("w_uq", [QR, MH * (NOPE + ROPE)])
    w_uq_rot = din("w_uq_rot", [QR, MH * ROPE])
    w_uk = din("w_uk", [KVR, MH * NOPE])
    w_uv = din("w_uv", [KVR, MH * MV])
    w_o = din("w_o", [D, D])
    w_up = din("w_up", [D, 2 * DFF])
    conv_w_pk = din("conv_w_pk", [128, 3, FC])
    conv_b_pk = din("conv_b_pk", [128, FC])
    w_down = din("w_down", [DFF, D])
    c_lat = din("c_lat", [NSS, PAST, KVR])
    c_kr = din("c_kr", [NSS, PAST, ROPE])
    c_gla = din("c_gla", [NSS, GH, GDK, GDV])
    c_conv = din("c_conv", [NSS, 2, 2 * DFF])

    o_y = dout("o_y", [NT, D])
    o_lat = dout("o_lat", [NT, KVR])
    o_kr = dout("o_kr", [NT, ROPE])
    o_gla_p = dout("o_gla_p", [GH, GDK, GDV])
    o_gla_s = dout("o_gla_s", [NSS, GH, GDK, GDV])
    o_conv = dout("o_conv", [2 + 2 * NSS, 2 * DFF])


    c_ident = din("c_ident", [128, 128])
    c_ones = din("c_ones", [128, 128])
    c_ls = din("c_ls", [128, 128])
    c_ui = din("c_ui", [128, 128])

    NKC = NPRE + NT + NSS * PAST
    NVT = NPT + 9 + NSS + NSS * 8
    kT_scr = dscr("kT_scr", [MH, 128, NKC], BF16)
    kpe_scr = dscr("kpe_scr", [64, NKC], BF16)
    v_scr = dscr("v_scr", [NVT, 128, MH * MV], BF16)
    qn_scr = dscr("qn_scr", [MH, 128, NT], BF16)
    qr_scr = dscr("qr_scr", [MH, 64, NT], BF16)
    xT_scr = dscr("xT_scr", [KC, 128, NT], F32)
    xn_scr = dscr("xn_scr", [KC, 128, NT], F32)
    xo_scr = dscr("xo_scr", [KC, 128, NT], F32)
    hpre_scr = dscr("hpre_scr", [NPRE // 512, 128, KC * 512], BF16)

    ps = [nc.alloc_psum_tensor("ps%d" % i, [128, 512], F32) for i in range(8)]
    PB = ["ps%d" % i for i in range(8)]

    class Phase:
        n = 0

        def __init__(self):
            Phase.n += 1
            self.id = Phase.n
            self.st = ExitStack()

        def sb(self, name, shape, dt):
            return self.st.enter_context(nc.sbuf_tensor("p%d_%s" % (self.id, name), list(shape), dt))

        def close(self):
            k.barrier()
            self.st.close()

    def wslab(dst, src2d, name, q="pool"):
        k.dma(q, dst, src2d.rearrange("(kc p) c -> p kc c", p=128), writes=[name], sb=name)

    identf = sb("identf", [128, 128], F32)
    identb = sb("identb", [128, 128], BF16)
    onesf = sb("onesf", [128, 128], F32)
    onesb = sb("onesb", [128, 128], BF16)
    lsf = sb("lsf", [128, 128], F32)
    uif = sb("uif", [128, 128], F32)
    k.dma("sp", identf[:], c_ident, writes=["identf"], sb="identf")
    k.dma("pool", identb[:], c_ident, writes=["identb"], sb="identb")
    k.dma("sp", onesf[:], c_ones, writes=["onesf"], sb="onesf")
    k.dma("pool", onesb[:], c_ones, writes=["onesb"], sb="onesb")
    k.dma("sp", lsf[:], c_ls, writes=["lsf"], sb="lsf")
    k.dma("sp", uif[:], c_ui, writes=["uif"], sb="uif")
    epsc = sb("epsc", [128, 1], F32)
    one1 = sb("one1", [128, 1], F32)
    k.op("pool", lambda e: e.memset(epsc[:], EPS), writes=["epsc"])
    k.op("pool", lambda e: e.memset(one1[:], 1.0), writes=["one1"])
    hT = sb("hT", [128, KC, NT], BF16)
    GM = {}

    def load_gmix(ph_):
        GM["t"] = ph_.sb("gmix", [128, D], F32)
        k.dma("sp", GM["t"][:], g_mix_bc, writes=["gmix"], sb="gmix")
    CONSTS = ["identf", "identb", "onesf", "onesb", "lsf", "uif", "epsc", "one1", "gmix"]

    def rsqrt_inplace(ap_out, ap_in, n_div, rd, wr, epsap):
        k.op("act", lambda e: e.activation(out=ap_out, in_=ap_in, func=AF.Sqrt, scale=1.0 / n_div, bias=epsap),
             reads=rd + ["epsc"], writes=wr)
        k.op("dve", lambda e: e.reciprocal(out=ap_out, in_=ap_out), reads=wr, writes=wr)

    def norm_T_tile(src_rows, n, dst, dname, X, XN, xb, ss):
        k.dma("sp", X[0:n, :], src_rows, writes=[XN], sb=XN)
        k.op("act", lambda e: e.activation(out=xb[0:n, :], in_=X[0:n, :], func=AF.Square, accum_out=ss[0:n, 0:1]),
             reads=[XN], writes=["xb", "ss"])
        rsqrt_inplace(ss[0:n, 1:2], ss[0:n, 0:1], D, ["ss"], ["ss"], epsc[0:n, :])
        k.op("dve", lambda e: e.scalar_tensor_tensor(out=xb[0:n, :], in0=X[0:n, :], scalar=ss[0:n, 1:2], in1=GM["t"][0:n, :],
                                                     op0=ALU.mult, op1=ALU.mult), reads=[XN, "ss", "gmix"], writes=["xb"])
        for half in range(2):
            pT = ps[half][:].bitcast(BF16)
            for j in range(8):
                kc = half * 8 + j
                k.op("pe", lambda e: e.transpose(out=pT[:, j * 128:j * 128 + n], in_=xb[0:n, kc * 128:(kc + 1) * 128],
                                                 identity=identb[0:n, 0:n]),
                     reads=["xb", "identb"], writes=[PB[half]], mark=(j == 7))
            src = pT.rearrange("p (j t) -> p j t", j=8)[:, :, 0:n]
            if half == 0:
                k.op("dve", lambda e: e.tensor_copy(out=dst[:, 0:8, 0:n], in_=src), reads=[PB[half]], writes=[dname])
            else:
                k.op("act", lambda e: e.copy(out=dst[:, 8:16, 0:n], in_=src), reads=[PB[half]], writes=[dname])

    ph = Phase()
    load_gmix(ph)
    xt = [ph.sb("xt%d" % i, [128, D], F32) for i in range(2)]
    xb = ph.sb("xb", [128, D], BF16)
    ss = ph.sb("ss", [128, 2], F32)
    xts = ph.sb("xts", [128, KC, 128], F32)
    tiles = [(t0, min(128, NT - t0)) for t0 in range(0, NT, 128)]
    for ti, (t0, n) in enumerate(tiles):
        X, XN = xt[ti % 2], "xt%d" % (ti % 2)
        norm_T_tile(x_win[t0:t0 + n, :], n, hT[:, :, t0:t0 + n], "hT", X, XN, xb, ss)
        for g in range(4):
            for j in range(4):
                kc = g * 4 + j
                k.op("pe", lambda e: e.transpose(out=ps[2 + g][:, j * 128:j * 128 + n], in_=X[0:n, kc * 128:(kc + 1) * 128],
                                                 identity=identf[0:n, 0:n]),
                     reads=[XN, "identf"], writes=[PB[2 + g]], mark=(j == 3))
            src = ps[2 + g][:].rearrange("p (j t) -> p j t", j=4)[:, :, 0:n]
            k.op("dve" if g % 2 == 0 else "pool" if False else "dve",
                 lambda e: e.tensor_copy(out=xts[:, g * 4:g * 4 + 4, 0:n], in_=src), reads=[PB[2 + g]], writes=["xts"])
        k.dma("sp", xT_scr[:, :, t0:t0 + n].rearrange("kc p t -> p kc t"), xts[:, :, 0:n], reads=["xts"], writes=["xT_scr"], sb="xts")
    ph.close()

    NKV = KVR + ROPE

    def mla_block(P, hsrc, hname, n, cosap, sinap, cname, latb_dst, kcol0, out_row0):
        wkv, wrot, sq, rstd, lat32, kr32, tmpk, kpeb, ostage, gkv = (P[x] for x in
            ("wkv", "wrot", "sq", "rstd", "lat32", "kr32", "tmpk", "kpeb", "ostage", "gkv"))
        for c in range(4):
            for kc in range(KC):
                k.op("pe", lambda e: e.matmul(ps[2 + c][:, 0:n], lhsT=wkv[:, kc, c * 128:(c + 1) * 128], rhs=hsrc[:, kc, :],
                                              start=(kc == 0), stop=(kc == KC - 1)),
                     reads=["wkv", hname], writes=[PB[2 + c]], mark=(kc == KC - 1))
            k.op("act", lambda e: e.activation(out=sq[:, c, 0:n], in_=ps[2 + c][:, 0:n], func=AF.Square), reads=[PB[2 + c]], writes=["sq"])
        for c in range(4):
            k.op("pe", lambda e: e.matmul(ps[6][:, 0:n], lhsT=onesf[:, :], rhs=sq[:, c, 0:n], start=(c == 0), stop=(c == 3)),
                 reads=["onesf", "sq"], writes=[PB[6]], mark=(c == 3))
        rsqrt_inplace(rstd[:, 0:n], ps[6][:, 0:n], KVR, [PB[6]], ["rstd"], epsc[:, :])
        for c in range(4):
            k.op("dve", lambda e: e.scalar_tensor_tensor(out=lat32[:, c, 0:n], in0=ps[2 + c][:, 0:n], scalar=gkv[:, c:c + 1],
                                                         in1=rstd[:, 0:n], op0=ALU.mult, op1=ALU.mult),
                 reads=[PB[2 + c], "gkv", "rstd"], writes=["lat32"])
        k.op("act", lambda e: e.copy(out=latb_dst[:, :, 0:n], in_=lat32[:, :, 0:n]), reads=["lat32"], writes=["latb"])
        for kc in range(KC):
            k.op("pe", lambda e: e.matmul(ps[7][:, 0:n], lhsT=wrot[:, kc, 0:128], rhs=hsrc[:, kc, :],
                                          start=(kc == 0), stop=(kc == KC - 1)),
                 reads=["wrot", hname], writes=[PB[7]], mark=(kc == KC - 1))
        k.op("dve", lambda e: e.tensor_tensor(out=kr32[:, 0:n], in0=ps[7][0:64, 0:n], in1=cosap, op=ALU.mult),
             reads=[PB[7], cname], writes=["kr32"])
        for kc in range(KC):
            k.op("pe", lambda e: e.matmul(ps[7][:, 0:n], lhsT=wrot[:, kc, 128:256], rhs=hsrc[:, kc, :], start=(kc == 0), stop=(kc == KC - 1)),
                 reads=["wrot", hname], writes=[PB[7]], mark=(kc == KC - 1))
        k.op("dve", lambda e: e.tensor_tensor(out=tmpk[:, 0:n], in0=ps[7][0:64, 0:n], in1=sinap, op=ALU.mult),
             reads=[PB[7], cname], writes=["tmpk"])
        k.op("dve", lambda e: e.tensor_tensor(out=kr32[:, 0:n], in0=kr32[:, 0:n], in1=tmpk[:, 0:n], op=ALU.add),
             reads=["kr32", "tmpk"], writes=["kr32"])
        k.op("act", lambda e: e.copy(out=kpeb[:, 0:n], in_=kr32[:, 0:n]), reads=["kr32"], writes=["kpeb"])
        k.dma("sp", kpe_scr[:, kcol0:kcol0 + n], kpeb[:, 0:n], reads=["kpeb"], writes=["kpe_scr"], sb="kpeb")
        if out_row0 is not None:
            for s0 in range(0, n, 128):
                m = min(128, n - s0)
                for c in range(4):
                    k.op("pe", lambda e: e.transpose(out=ps[0][0:m, c * 128:(c + 1) * 128], in_=lat32[:, c, s0:s0 + m], identity=identf[:, :]),
                         reads=["lat32", "identf"], writes=[PB[0]], mark=(c == 3))
                k.op("pe", lambda e: e.transpose(out=ps[1][0:m, 0:64], in_=kr32[0:64, s0:s0 + m], identity=identf[0:64, 0:64]),
                     reads=["kr32", "identf"], writes=[PB[1]])
                k.op("dve", lambda e: e.tensor_copy(out=ostage[0:m, 0:KVR], in_=ps[0][0:m, :]), reads=[PB[0]], writes=["ostage"])
                k.op("act", lambda e: e.copy(out=ostage[0:m, KVR:NKV], in_=ps[1][0:m, 0:64]), reads=[PB[1]], writes=["ostage"])
                r0 = out_row0 + s0
                k.dma("sp", o_lat[r0:r0 + m, :], ostage[0:m, 0:KVR], reads=["ostage"], writes=["o_lat"], sb="ostage")
                k.dma("sp", o_kr[r0:r0 + m, :], ostage[0:m, KVR:NKV], reads=["ostage"], writes=["o_kr"], sb="ostage_b")

    kgen_i = [0]
    vgen_i = [0]

    def k_gen(P, latb, n, kcol0):
        wuk, kst = P["wuk"], P["kst"]
        for h in range(MH):
            b = 2 + (kgen_i[0] % 4)
            si = kgen_i[0] % len(kst)
            kgen_i[0] += 1
            for c in range(4):
                k.op("pe", lambda e: e.matmul(ps[b][:, 0:n], lhsT=wuk[:, c, h * NOPE:(h + 1) * NOPE], rhs=latb[:, c, 0:n],
                                              start=(c == 0), stop=(c == 3)), reads=["wuk", "latb"], writes=[PB[b]], mark=(c == 3))
            if si % 2 == 0:
                k.op("dve", lambda e: e.tensor_copy(out=kst[si][:, 0:n], in_=ps[b][:, 0:n]), reads=[PB[b]], writes=["kst%d" % si])
            else:
                k.op("act", lambda e: e.copy(out=kst[si][:, 0:n], in_=ps[b][:, 0:n]), reads=[PB[b]], writes=["kst%d" % si])
            k.dma("sp", kT_scr[h, :, kcol0:kcol0 + n], kst[si][:, 0:n], reads=["kst%d" % si], writes=["kT_scr"], sb="kst%d" % si)

    def v_gen(P, latb, t0, m, vt):
        wuv = P["wuv"]
        vi = vgen_i[0] % len(P["vst"])
        vgen_i[0] += 1
        vst, vsn = P["vst"][vi], "vst%d" % vi
        for cg in range(4):
            b = 2 + cg
            for c in range(4):
                k.op("pe", lambda e: e.matmul(ps[b][0:m, :], lhsT=latb[:, c, t0:t0 + m], rhs=wuv[:, c, cg * 512:(cg + 1) * 512],
                                              start=(c == 0), stop=(c == 3)), reads=["wuv", "latb"], writes=[PB[b]], mark=(c == 3))
            if cg % 2 == 0:
                k.op("dve", lambda e: e.tensor_copy(out=vst[0:m, cg * 512:(cg + 1) * 512], in_=ps[b][0:m, :]), reads=[PB[b]], writes=[vsn])
            else:
                k.op("act", lambda e: e.copy(out=vst[0:m, cg * 512:(cg + 1) * 512], in_=ps[b][0:m, :]), reads=[PB[b]], writes=[vsn])
        k.dma("sp", v_scr[vt, 0:m, :], vst[0:m, :], reads=[vsn], writes=["v_scr"], sb=vsn)

    def mla_phase_tensors(ph, n_lat):
        P = {}
        P["wkv"] = ph.sb("wkv", [128, KC, NKV], BF16)
        P["wrot"] = ph.sb("wrot", [128, KC, 256], BF16)
        k.op("pool", lambda e: e.memset(P["wrot"][:], 0.0), writes=["wrot"])
        wslab(P["wkv"][:], w_in[:, O_CKV:O_CKV + NKV], "wkv")
        wslab(P["wrot"][:, :, 0:ROPE], w_in[:, O_KPE:O_KPE + ROPE], "wrot")
        wslab(P["wrot"][:, :, 128:128 + ROPE], w_kpe_rot, "wrot")
        P["wuk"] = ph.sb("wuk", [128, 4, MH * NOPE], BF16)
        P["wuv"] = ph.sb("wuv", [128, 4, MH * MV], BF16)
        wslab(P["wuk"][:], w_uk, "wuk")
        wslab(P["wuv"][:], w_uv, "wuv")
        P["gkv"] = ph.sb("gkv", [128, 4], F32)
        k.dma("sp", P["gkv"][:], g_kv_pk, writes=["gkv"], sb="gkv")
        P["sq"] = ph.sb("sq", [128, 4, 512], F32)
        P["rstd"] = ph.sb("rstd", [128, 512], F32)
        P["lat32"] = ph.sb("lat32", [128, 4, 512], F32)
        P["kr32"] = ph.sb("kr32", [64, 512], F32)
        P["tmpk"] = ph.sb("tmpk", [64, 512], F32)
        P["kpeb"] = ph.sb("kpeb", [64, 512], BF16)
        P["ostage"] = ph.sb("ostage", [128, NKV], F32) if n_lat == NT else None
        P["kst"] = [ph.sb("kst%d" % i, [128, 512], BF16) for i in range(4)]
        P["vst"] = [ph.sb("vst%d" % i, [128, MH * MV], BF16) for i in range(2)]
        P["latb"] = ph.sb("latb", [128, 4, n_lat], BF16)
        return P

    ph = Phase()
    P = mla_phase_tensors(ph, NT)
    cosw = ph.sb("cosw", [64, NT], F32)
    sinw = ph.sb("sinw", [64, NT], F32)
    k.dma("sp", cosw[:], cos_win, writes=["cosw"], sb="cosw")
    k.dma("sp", sinw[:], sin_win, writes=["cosw"], sb="cosw")
    for p0 in (range(0, NT, 512) if K_P2 else []):
        n = min(512, NT - p0)
        mla_block(P, hT[:, :, p0:p0 + n], "hT", n, cosw[:, p0:p0 + n], sinw[:, p0:p0 + n], "cosw",
                  P["latb"][:, :, p0:p0 + n], NPRE + p0, p0)
        k_gen(P, P["latb"][:, :, p0:p0 + n], n, NPRE + p0)
    wtiles = [(0, 16)] + [(16 + 128 * m, 128) for m in range(8)] + [(WINP + TS * s_, TS) for s_ in range(NSS)]
    for wi, (t0, m) in (enumerate(wtiles) if K_P2 >= 2 else []):
        v_gen(P, P["latb"], t0, m, NPT + wi)
    ph.close()

    ph = Phase()
    load_gmix(ph)
    P = mla_phase_tensors(ph, 512)
    xt = [ph.sb("xt%d" % i, [128, D], F32) for i in range(2)]
    xb = ph.sb("xb", [128, D], BF16)
    ss = ph.sb("ss", [128, 2], F32)
    hTp = ph.sb("hTp", [128, KC, 512], BF16)
    cosp = ph.sb("cosp", [64, 512], F32)
    sinp = ph.sb("sinp", [64, 512], F32)
    for blk in range(NPRE_BLK):
        b0 = blk * 512
        k.dma("sp", cosp[:], cos_pre[:, b0:b0 + 512], writes=["cosp"], sb="cosp")
        k.dma("sp", sinp[:], sin_pre[:, b0:b0 + 512], writes=["cosp"], sb="cosp")
        for j in range(4):
            ti = blk * 4 + j
            norm_T_tile(x_pre[ti * 128:(ti + 1) * 128, :], 128, hTp[:, :, j * 128:(j + 1) * 128], "hTp", xt[ti % 2], "xt%d" % (ti % 2), xb, ss)
        k.dma("sp", hpre_scr[blk], hTp[:].rearrange("p kc t -> p (kc t)"), reads=["hTp"], writes=["hpre_scr"], sb="hTp")
        mla_block(P, hTp[:, :, :], "hTp", 512, cosp[:, :], sinp[:, :], "cosp", P["latb"][:, :, :], b0, None)
        k_gen(P, P["latb"], 512, b0)
        for j in range(4):
            v_gen(P, P["latb"], j * 128, 128, blk * 4 + j)
    ph.close()

    ph = Phase()
    P = {}
    P["wuk"] = ph.sb("wuk", [128, 4, MH * NOPE], BF16)
    P["wuv"] = ph.sb("wuv", [128, 4, MH * MV], BF16)
    wslab(P["wuk"][:], w_uk, "wuk")
    wslab(P["wuv"][:], w_uv, "wuv")
    P["kst"] = [ph.sb("kst%d" % i, [128, 512], BF16) for i in range(4)]
    P["vst"] = [ph.sb("vst%d" % i, [128, MH * MV], BF16) for i in range(2)]
    P["latb"] = ph.sb("latb", [128, 4, PAST], BF16)
    ctm = [ph.sb("ctm%d" % i, [128, NKV + 64], BF16) for i in range(2)]
    for i in range(2):
        k.op("pool", lambda e: e.memset(ctm[i][:], 0.0), writes=["ctm%d" % i])
    kpec = ph.sb("kpec", [64, PAST], BF16)
    ctf = [ph.sb("ctf%d" % i, [128, NKV], F32) for i in range(2)]
    for s_ in range(NSS_RUN):
        kc0 = NPRE + NT + s_ * PAST
        for j in range(PAST // 128):
            T, TN = ctm[j % 2], "ctm%d" % (j % 2)
            CF, CFN = ctf[j % 2], "ctf%d" % (j % 2)
            k.dma("sp", CF[:, 0:KVR], c_lat[s_, j * 128:(j + 1) * 128, :], writes=[CFN], sb=CFN)
            k.dma("sp", CF[:, KVR:NKV], c_kr[s_, j * 128:(j + 1) * 128, :], writes=[CFN], sb=CFN)
            if not (K_3B & 16):
                continue
            k.op("dve", lambda e: e.tensor_copy(out=T[:, 0:NKV], in_=CF[:, :]), reads=[CFN], writes=[TN])
            if not (K_3B & 32):
                continue
            pT = ps[j % 2][:].bitcast(BF16)
            for c in range(4):
                k.op("pe", lambda e: e.transpose(out=pT[:, c * 128:(c + 1) * 128], in_=T[:, c * 128:(c + 1) * 128], identity=identb[:, :]),
                     reads=[TN, "identb"], writes=[PB[j % 2]], mark=(c == 3))
            if K_3B & 64:
                k.op("pe", lambda e: e.transpose(out=pT[:, 512:640], in_=T[:, 512:640], identity=identb[:, :]),
                     reads=[TN, "identb"], writes=[PB[j % 2]])
            k.op("dve", lambda e: e.tensor_copy(out=P["latb"][:, :, j * 128:(j + 1) * 128],
                                                in_=pT[:, 0:512].rearrange("p (c t) -> p c t", c=4)), reads=[PB[j % 2]], writes=["latb"])
            if K_3B & 64:
                k.op("dve", lambda e: e.tensor_copy(out=kpec[:, j * 128:(j + 1) * 128], in_=pT[0:64, 512:640]), reads=[PB[j % 2]], writes=["kpec"])
        if K_3B & 2:
            k.dma("sp", kpe_scr[:, kc0:kc0 + PAST], kpec[:, :], reads=["kpec"], writes=["kpe_scr"], sb="kpec")
        for p0 in (range(0, PAST, 512) if K_3B & 4 else []):
            k_gen(P, P["latb"][:, :, p0:p0 + 512], 512, kc0 + p0)
        for j in (range(PAST // 128) if K_3B & 8 else []):
            v_gen(P, P["latb"], j * 128, 128, NPT + 9 + NSS + s_ * 8 + j)
    ph.close()

    Spre_scr = dscr("Spre_scr", [GH, GDK, GDV], F32)

    def gla_common(ph):
        G = {}
        G["bab"] = ph.sb("bab", [128, 1024], F32)
        k.dma("sp", G["bab"][:], b_a_bc, writes=["bab"], sb="bab")
        G["wa"] = ph.sb("wa", [128, KC, 128], BF16)
        k.op("pool", lambda e: e.memset(G["wa"][:], 0.0), writes=["wa"])
        wslab(G["wa"][:, :, 0:16], w_in[:, O_A:O_A + 16], "wa")
        G["wa2b"] = ph.sb("wa2b", [128, 1024], BF16)
        k.op("pool", lambda e: e.memset(G["wa2b"][:], 0.0), writes=["wa2b"])
        k.dma("pool", G["wa2b"][0:16, :], w_a2, writes=["wa2b"], sb="wa2b")
        G["agT"] = ph.sb("agT", [128, 128], BF16)
        for nm, shp, dt_ in (("zz", [128, 256], F32), ("la", [128, 256], F32), ("er", [128, 256], F32), ("dec", [128, 2], F32),
                             ("kend", [128, 256], BF16), ("vb", [128, 512], BF16)):
            G[nm] = [ph.sb("%s%d" % (nm, i), shp, dt_) for i in range(2)]
        return G

    def gla_ag(G, hsrc, hname, C):
        for kc in range(KC):
            k.op("pe", lambda e: e.matmul(ps[0][:, 0:C], lhsT=G["wa"][:, kc, :], rhs=hsrc[:, kc, :], start=(kc == 0), stop=(kc == KC - 1)),
                 reads=["wa", hname], writes=[PB[0]], mark=(kc == KC - 1))
        k.op("act", lambda e: e.copy(out=G["agT"][:, 0:C], in_=ps[0][:, 0:C]), reads=[PB[0]], writes=["agT"])

    def gla_state_step(G, hsrc, hname, C, h, wk_t, wkn, wv_t, wvn, S_t, SN, vcol, mid=None, slot=0, stages="ABC"):
        zz, la, er, dec, kend, vb = (G[x][slot] for x in ("zz", "la", "er", "dec", "kend", "vb"))
        zzn, lan, ern, decn, kendn, vbn = ("%s%d" % (x, slot) for x in ("zz", "la", "er", "dec", "kend", "vb"))
        BK, BV, BR = (1, 2, 3) if slot == 0 else (4, 5, 6)
        if "A" in stages:
            gla_stage_a(G, hsrc, hname, C, h, wk_t, wkn, wv_t, wvn, vcol, slot, BK, BV)
        if "B" in stages:
            gla_stage_b(G, C, vcol, slot, BK, BV, BR)
            if mid is not None:
                mid()
        if "C" in stages:
            gla_stage_c(G, C, S_t, SN, slot)

    def gla_stage_a(G, hsrc, hname, C, h, wk_t, wkn, wv_t, wvn, vcol, slot, BK, BV):
        zz, la, er, dec, kend, vb = (G[x][slot] for x in ("zz", "la", "er", "dec", "kend", "vb"))
        zzn, lan, ern, decn, kendn, vbn = ("%s%d" % (x, slot) for x in ("zz", "la", "er", "dec", "kend", "vb"))
        for kc in range(KC):
            k.op("pe", lambda e: e.matmul(ps[BK][0:C, 0:GDK], lhsT=hsrc[:, kc, :], rhs=wk_t[:, kc, :], start=(kc == 0), stop=(kc == KC - 1)),
                 reads=[hname, wkn], writes=[PB[BK]], mark=(kc == KC - 1))
        for kc in range(KC):
            k.op("pe", lambda e: e.matmul(ps[BV][0:C, 0:GDV], lhsT=hsrc[:, kc, :], rhs=wv_t[:, kc, :], start=(kc == 0), stop=(kc == KC - 1)),
                 reads=[hname, wvn], writes=[PB[BV]], mark=(kc == KC - 1))
        k.op("pe", lambda e: e.matmul(ps[BK][0:C, GDK:2 * GDK], lhsT=G["agT"][:, 0:C], rhs=G["wa2b"][:, h * GDK:(h + 1) * GDK], start=True, stop=True),
             reads=["agT", "wa2b"], writes=[PB[BK]])
        k.op("dve", lambda e: e.tensor_tensor(out=zz[0:C, :], in0=ps[BK][0:C, GDK:2 * GDK], in1=G["bab"][0:C, h * GDK:(h + 1) * GDK], op=ALU.add),
             reads=[PB[BK], "bab"], writes=[zzn])
        k.op("act", lambda e: e.activation(out=zz[0:C, :], in_=zz[0:C, :], func=AF.Exp, scale=-1.0), reads=[zzn], writes=[zzn])
        k.op("act", lambda e: e.activation(out=zz[0:C, :], in_=zz[0:C, :], func=AF.Ln, bias=one1[0:C, :]), reads=[zzn, "one1"], writes=[zzn])
        k.op("dve", lambda e: e.tensor_scalar(out=la[0:C, :], in0=zz[0:C, :], scalar1=-1.0 / 16.0, scalar2=-5.0, op0=ALU.mult, op1=ALU.max),
             reads=[zzn], writes=[lan])
        if vcol is not None:
            k.op("dve", lambda e: e.tensor_scalar(out=la[0:C, :], in0=la[0:C, :], scalar1=vcol, scalar2=None, op0=ALU.mult),
                 reads=[lan, "pval"], writes=[lan])

    def gla_stage_b(G, C, vcol, slot, BK, BV, BR):
        zz, la, er, dec, kend, vb = (G[x][slot] for x in ("zz", "la", "er", "dec", "kend", "vb"))
        zzn, lan, ern, decn, kendn, vbn = ("%s%d" % (x, slot) for x in ("zz", "la", "er", "dec", "kend", "vb"))
        k.op("pe", lambda e: e.matmul(ps[BR][0:C, 0:GDK], lhsT=lsf[0:C, 0:C], rhs=la[0:C, :], start=True, stop=True),
             reads=["lsf", lan], writes=[PB[BR]])
        for dc in range(2):
            k.op("pe", lambda e: e.matmul(ps[BR][:, GDK + dc:GDK + dc + 1], lhsT=la[0:C, dc * 128:(dc + 1) * 128], rhs=onesf[0:C, 0:1], start=True, stop=True),
                 reads=[lan, "onesf"], writes=[PB[BR]], mark=(dc == 1))
        k.op("act", lambda e: e.activation(out=er[0:C, :], in_=ps[BR][0:C, 0:GDK], func=AF.Exp), reads=[PB[BR]], writes=[ern])
        k.op("act", lambda e: e.activation(out=dec[:, :], in_=ps[BR][:, GDK:GDK + 2], func=AF.Exp), reads=[PB[BR]], writes=[decn])
        if vcol is not None:
            k.op("dve", lambda e: e.scalar_tensor_tensor(out=kend[0:C, :], in0=ps[BK][0:C, 0:GDK], scalar=vcol, in1=er[0:C, :],
                                                         op0=ALU.mult, op1=ALU.mult), reads=[PB[BK], ern, "pval"], writes=[kendn])
        else:
            k.op("dve", lambda e: e.tensor_tensor(out=kend[0:C, :], in0=ps[BK][0:C, 0:GDK], in1=er[0:C, :], op=ALU.mult),
                 reads=[PB[BK], ern], writes=[kendn])
        k.op("act", lambda e: e.copy(out=vb[0:C, :], in_=ps[BV][0:C, 0:GDV]), reads=[PB[BV]], writes=[vbn])

    def gla_stage_c(G, C, S_t, SN, slot):
        zz, la, er, dec, kend, vb = (G[x][slot] for x in ("zz", "la", "er", "dec", "kend", "vb"))
        zzn, lan, ern, decn, kendn, vbn = ("%s%d" % (x, slot) for x in ("zz", "la", "er", "dec", "kend", "vb"))
        for dc in range(2):
            BD = 7 if dc == 0 else 0
            k.op("pe", lambda e: e.matmul(ps[BD][:, 0:GDV], lhsT=kend[0:C, dc * 128:(dc + 1) * 128], rhs=vb[0:C, :], start=True, stop=True),
                 reads=[kendn, vbn], writes=[PB[BD]])
            k.op("dve", lambda e: e.scalar_tensor_tensor(out=S_t[:, dc, :], in0=S_t[:, dc, :], scalar=dec[:, dc:dc + 1],
                                                         in1=ps[BD][:, 0:GDV], op0=ALU.mult, op1=ALU.add),
                 reads=[SN, decn, PB[BD]], writes=[SN])

    ph = Phase()
    G = gla_common(ph)
    hTps = [ph.sb("hTp%d" % i, [128, KC, 512], BF16) for i in range(2)]
    pval = ph.sb("pval", [128, NPT], F32)
    k.dma("sp", pval[:], pre_valid, writes=["pval"], sb="pval")
    wk = [ph.sb("wk%d" % i, [128, KC, GDK], BF16) for i in range(2)]
    wv = [ph.sb("wv%d" % i, [128, KC, GDV], BF16) for i in range(2)]
    S32 = [ph.sb("S32_%d" % i, [128, 2, GDV], F32) for i in range(2)]
    for pair in range(2):
        for hi in range(2):
            h = pair * 2 + hi
            wslab(wk[hi][:], w_in[:, O_K + h * GDK:O_K + (h + 1) * GDK], "wk%d" % hi)
            wslab(wv[hi][:], w_in[:, O_V + h * GDV:O_V + (h + 1) * GDV], "wv%d" % hi)
            k.op("pool", lambda e: e.memset(S32[hi][:], 0.0), writes=["S32_%d" % hi])
        for ti in range(NPT_RUN):
            blk, jt = ti // 4, ti % 4
            hTn = "hTp%d" % (blk % 2)
            if jt == 0:
                k.dma("sp", hTps[blk % 2][:].rearrange("p kc t -> p (kc t)"), hpre_scr[blk], writes=[hTn], sb=hTn)
            hTp = hTps[blk % 2][:, :, jt * 128:(jt + 1) * 128]
            gla_ag(G, hTp, hTn, 128)
            for stg in "ABC":
                for hi in range(2):
                    gla_state_step(G, hTp, hTn, 128, pair * 2 + hi, wk[hi], "wk%d" % hi, wv[hi], "wv%d" % hi,
                                   S32[hi], "S32_%d" % hi, pval[:, ti:ti + 1], slot=hi, stages=stg)
        for hi in range(2):
            h = pair * 2 + hi
            k.dma("sp", Spre_scr[h].rearrange("(dc p) e -> p dc e", p=128), S32[hi][:], reads=["S32_%d" % hi], writes=["Spre_scr"], sb="S32_%d" % hi)
    ph.close()

    mergedT = sb("mergedT", [128, KC, NT], BF16)
    ph = Phase()
    G = gla_common(ph)
    ggla = ph.sb("ggla", [128, 4], F32)
    k.dma("sp", ggla[:], g_gla_pk, writes=["ggla"], sb="ggla")
    wk1 = ph.sb("wk", [128, KC, GDK], BF16)
    wv1 = ph.sb("wv", [128, KC, GDV], BF16)
    wq_t = ph.sb("wq", [128, KC, GDK], BF16)
    wr_t = ph.sb("wr", [128, KC, GDV], BF16)
    wga_t = ph.sb("wga", [128, KC, GDV], BF16)
    S1 = ph.sb("S1", [128, 2, GDV], F32)
    Sb = ph.sb("Sb", [128, 2, GDV], BF16)
    qTh = ph.sb("qTh", [128, 2, NT], BF16)
    kTf = ph.sb("kTf", [128, 2, NT], BF16)
    rT = ph.sb("rT", [128, 4, NT], BF16)
    srT = ph.sb("srT", [128, 4, NT], BF16)
    gtmp = ph.sb("gtmp", [128, 512], F32)
    eb = ph.sb("eb", [128, 2, 128], F32)
    enb = ph.sb("enb", [128, 2, 128], F32)
    qd = ph.sb("qd", [128, 2, 128], BF16)
    ki = ph.sb("ki", [128, 2, 128], BF16)
    attb = ph.sb("attb", [128, 128], BF16)
    osq = ph.sb("osq", [128, 512], F32)
    orst = ph.sb("orst", [128, 128], F32)
    otmp = ph.sb("otmp", [128, 128], F32)
    pieces = [(p0, min(512, NT - p0)) for p0 in range(0, NT, 512)]
    gemm_i = [0]

    def gemm_fm(wt, wname, col0, post):
        for (p0, n) in pieces:
            b = 5 + gemm_i[0] % 3
            gemm_i[0] += 1
            for kc in range(KC):
                k.op("pe", lambda e: e.matmul(ps[b][:, 0:n], lhsT=wt[:, kc, col0:col0 + 128], rhs=hT[:, kc, p0:p0 + n],
                                              start=(kc == 0), stop=(kc == KC - 1)), reads=[wname, "hT"], writes=[PB[b]], mark=(kc == KC - 1))
            post(ps[b][:, 0:n], PB[b], p0, n)

    def gla_out(C, c0, h):
        la = G["la"][0]
        vb = G["vb"][0]
        for dc in range(2):
            k.op("pe", lambda e: e.matmul(ps[3][:, dc * C:(dc + 1) * C], lhsT=la[0:C, dc * 128:(dc + 1) * 128], rhs=uif[0:C, 0:C], start=True, stop=True),
                 reads=["la0", "uif"], writes=[PB[3]], mark=(dc == 1))
        pv = ps[3][:, 0:2 * C].rearrange("p (d t) -> p d t", d=2)
        k.op("act", lambda e: e.activation(out=eb[:, :, 0:C], in_=pv, func=AF.Exp), reads=[PB[3]], writes=["eb"])
        k.op("act", lambda e: e.activation(out=enb[:, :, 0:C], in_=pv, func=AF.Exp, scale=-1.0), reads=[PB[3]], writes=["enb"])
        k.op("dve", lambda e: e.tensor_tensor(out=qd[:, :, 0:C], in0=qTh[:, :, c0:c0 + C], in1=eb[:, :, 0:C], op=ALU.mult),
             reads=["qTh", "eb"], writes=["qd"])
        k.op("dve", lambda e: e.tensor_tensor(out=ki[:, :, 0:C], in0=kTf[:, :, c0:c0 + C], in1=enb[:, :, 0:C], op=ALU.mult),
             reads=["kTf", "enb"], writes=["ki"])
        for dc in range(2):
            k.op("pe", lambda e: e.matmul(ps[0][0:C, 0:C], lhsT=ki[:, dc, 0:C], rhs=qd[:, dc, 0:C], start=(dc == 0), stop=(dc == 1)),
                 reads=["ki", "qd"], writes=[PB[0]], mark=(dc == 1))
        k.op("dve", lambda e: e.tensor_tensor(out=attb[0:C, 0:C], in0=ps[0][0:C, 0:C], in1=uif[0:C, 0:C], op=ALU.mult),
             reads=[PB[0], "uif"], writes=["attb"])
        for ec in range(4):
            k.op("pe", lambda e: e.matmul(ps[1][:, ec * C:(ec + 1) * C], lhsT=vb[0:C, ec * 128:(ec + 1) * 128], rhs=attb[0:C, 0:C], start=True, stop=False),
                 reads=["vb0", "attb"], writes=[PB[1]], mark=False)
            for dc in range(2):
                k.op("pe", lambda e: e.matmul(ps[1][:, ec * C:(ec + 1) * C], lhsT=Sb[:, dc, ec * 128:(ec + 1) * 128], rhs=qd[:, dc, 0:C],
                                              start=False, stop=(dc == 1)), reads=["Sb", "qd"], writes=[PB[1]], mark=(dc == 1 and ec == 3))
        k.op("act", lambda e: e.activation(out=osq[:, 0:4 * C], in_=ps[1][:, 0:4 * C], func=AF.Square), reads=[PB[1]], writes=["osq"])
        for ec in range(4):
            k.op("pe", lambda e: e.matmul(ps[4][:, 0:C], lhsT=onesf[:, :], rhs=osq[:, ec * C:(ec + 1) * C], start=(ec == 0), stop=(ec == 3)),
                 reads=["onesf", "osq"], writes=[PB[4]], mark=(ec == 3))
        rsqrt_inplace(orst[:, 0:C], ps[4][:, 0:C], GDV, [PB[4]], ["orst"], epsc[:, :])
        for ec in range(4):
            k.op("dve", lambda e: e.scalar_tensor_tensor(out=otmp[:, 0:C], in0=ps[1][:, ec * C:(ec + 1) * C], scalar=ggla[:, ec:ec + 1],
                                                         in1=orst[:, 0:C], op0=ALU.mult, op1=ALU.mult), reads=[PB[1], "ggla", "orst"], writes=["otmp"])
            k.op("dve", lambda e: e.tensor_tensor(out=mergedT[:, 4 * h + ec, c0:c0 + C], in0=otmp[:, 0:C], in1=srT[:, ec, c0:c0 + C], op=ALU.mult),
                 reads=["otmp", "srT"], writes=["mergedT"])

    for h in range(GH_RUN):
        wslab(wk1[:], w_in[:, O_K + h * GDK:O_K + (h + 1) * GDK], "wk")
        wslab(wv1[:], w_in[:, O_V + h * GDV:O_V + (h + 1) * GDV], "wv")
        k.dma("sp", S1[:], Spre_scr[h].rearrange("(dc p) e -> p dc e", p=128), reads=["Spre_scr"], writes=["S1"], sb="S1")
        wslab(wq_t[:], w_in[:, O_Q + h * GDK:O_Q + (h + 1) * GDK], "wq")
        wslab(wr_t[:], w_in[:, O_R + h * GDV:O_R + (h + 1) * GDV], "wr")
        wslab(wga_t[:], w_in[:, O_GA + h * GDV:O_GA + (h + 1) * GDV], "wga")
        for dc in range(2):
            gemm_fm(wq_t, "wq", dc * 128, lambda pa, pn, p0, n, dc=dc: k.op(
                "act", lambda e: e.activation(out=qTh[:, dc, p0:p0 + n], in_=pa, func=AF.Copy, scale=GDK ** -0.5), reads=[pn], writes=["qTh"]))
            gemm_fm(wk1, "wk", dc * 128, lambda pa, pn, p0, n, dc=dc: k.op(
                "dve", lambda e: e.tensor_copy(out=kTf[:, dc, p0:p0 + n], in_=pa), reads=[pn], writes=["kTf"]))
        for ec in range(4):
            gemm_fm(wr_t, "wr", ec * 128, lambda pa, pn, p0, n, ec=ec: k.op(
                "act", lambda e: e.activation(out=rT[:, ec, p0:p0 + n], in_=pa, func=AF.Silu), reads=[pn], writes=["rT"]))
        for ec in range(4):
            def post_ga(pa, pn, p0, n, ec=ec):
                k.op("act", lambda e: e.activation(out=gtmp[:, 0:n], in_=pa, func=AF.Sigmoid), reads=[pn], writes=["gtmp"])
                k.op("dve", lambda e: e.tensor_tensor(out=srT[:, ec, p0:p0 + n], in0=rT[:, ec, p0:p0 + n], in1=gtmp[:, 0:n], op=ALU.mult),
                     reads=["rT", "gtmp"], writes=["srT"])
            gemm_fm(wga_t, "wga", ec * 128, post_ga)
        chunks = [(0, 16)] + [(16 + 128 * m, 128) for m in range(8)]
        for (c0, C) in chunks:
            gla_ag(G, hT[:, :, c0:c0 + C], "hT", C)
            k.op("act", lambda e: e.copy(out=Sb[:], in_=S1[:]), reads=["S1"], writes=["Sb"])
            gla_state_step(G, hT[:, :, c0:c0 + C], "hT", C, h, wk1, "wk", wv1, "wv", S1, "S1", None,
                           mid=lambda C=C, c0=c0: gla_out(C, c0, h))
        k.dma("sp", o_gla_p[h].rearrange("(dc p) e -> p dc e", p=128), S1[:], reads=["S1"], writes=["o_gla_p"], sb="S1")
        for s_ in range(NSS):
            c0 = WINP + TS * s_
            k.dma("sp", S1[:], c_gla[s_, h].rearrange("(dc p) e -> p dc e", p=128), writes=["S1"], sb="S1")
            gla_ag(G, hT[:, :, c0:c0 + TS], "hT", TS)
            k.op("act", lambda e: e.copy(out=Sb[:], in_=S1[:]), reads=["S1"], writes=["Sb"])
            gla_state_step(G, hT[:, :, c0:c0 + TS], "hT", TS, h, wk1, "wk", wv1, "wv", S1, "S1", None,
                           mid=lambda c0=c0: gla_out(TS, c0, h))
            k.dma("sp", o_gla_s[s_, h].rearrange("(dc p) e -> p dc e", p=128), S1[:], reads=["S1"], writes=["o_gla_s"], sb="S1")
    ph.close()

    ph = Phase()
    wcq = ph.sb("wcq", [128, KC, QR], BF16)
    wslab(wcq[:], w_in[:, O_CQ:O_CQ + QR], "wcq")
    wuq = ph.sb("wuq", [128, 4, MH * (NOPE + ROPE)], BF16)
    wslab(wuq[:], w_uq, "wuq")
    wuqr = ph.sb("wuqr", [128, 4, MH * ROPE], BF16)
    wslab(wuqr[:], w_uq_rot, "wuqr")
    gq = ph.sb("gq", [128, 4], F32)
    k.dma("sp", gq[:], g_q_pk, writes=["gq"], sb="gq")
    cosw = ph.sb("cosw", [64, NT], F32)
    sinw = ph.sb("sinw", [64, NT], F32)
    k.dma("sp", cosw[:], cos_win, writes=["cosw"], sb="cosw")
    k.dma("sp", sinw[:], sin_win, writes=["cosw"], sb="cosw")
    sq = ph.sb("sq", [128, 4, 512], F32)
    rstd = ph.sb("rstd", [128, 512], F32)
    qnb = ph.sb("qnb", [128, 4, 512], BF16)
    qst = [ph.sb("qst%d" % i, [128, 512], BF16) for i in range(2)]
    qrs = [ph.sb("qrs%d" % i, [64, 512], BF16) for i in range(2)]
    t1 = ph.sb("t1", [64, 512], F32)
    t2 = ph.sb("t2", [64, 512], F32)
    for (p0, n) in [(p0, min(512, NT - p0)) for p0 in range(0, NT, 512)]:
        for c in range(4):
            for kc in range(KC):
                k.op("pe", lambda e: e.matmul(ps[2 + c][:, 0:n], lhsT=wcq[:, kc, c * 128:(c + 1) * 128], rhs=hT[:, kc, p0:p0 + n],
                                              start=(kc == 0), stop=(kc == KC - 1)), reads=["wcq", "hT"], writes=[PB[2 + c]], mark=(kc == KC - 1))
            k.op("act", lambda e: e.activation(out=sq[:, c, 0:n], in_=ps[2 + c][:, 0:n], func=AF.Square), reads=[PB[2 + c]], writes=["sq"])
        for c in range(4):
            k.op("pe", lambda e: e.matmul(ps[6][:, 0:n], lhsT=onesf[:, :], rhs=sq[:, c, 0:n], start=(c == 0), stop=(c == 3)),
                 reads=["onesf", "sq"], writes=[PB[6]], mark=(c == 3))
        rsqrt_inplace(rstd[:, 0:n], ps[6][:, 0:n], QR, [PB[6]], ["rstd"], epsc[:, :])
        for c in range(4):
            k.op("dve", lambda e: e.scalar_tensor_tensor(out=qnb[:, c, 0:n], in0=ps[2 + c][:, 0:n], scalar=gq[:, c:c + 1], in1=rstd[:, 0:n],
                                                         op0=ALU.mult, op1=ALU.mult), reads=[PB[2 + c], "gq", "rstd"], writes=["qnb"])
        for h in range(MH):
            si = h % 2
            o0 = h * (NOPE + ROPE)
            for c in range(4):
                k.op("pe", lambda e: e.matmul(ps[si][:, 0:n], lhsT=wuq[:, c, o0:o0 + NOPE], rhs=qnb[:, c, 0:n], start=(c == 0), stop=(c == 3)),
                     reads=["wuq", "qnb"], writes=[PB[si]], mark=(c == 3))
            k.op("act", lambda e: e.copy(out=qst[si][:, 0:n], in_=ps[si][:, 0:n]), reads=[PB[si]], writes=["qst%d" % si])
            k.dma("sp", qn_scr[h, :, p0:p0 + n], qst[si][:, 0:n], reads=["qst%d" % si], writes=["qn_scr"], sb="qst%d" % si)
            for c in range(4):
                k.op("pe", lambda e: e.matmul(ps[7][0:64, 0:n], lhsT=wuq[:, c, o0 + NOPE:o0 + NOPE + ROPE], rhs=qnb[:, c, 0:n], start=(c == 0), stop=(c == 3)),
                     reads=["wuq", "qnb"], writes=[PB[7]], mark=(c == 3))
            k.op("dve", lambda e: e.tensor_tensor(out=t1[:, 0:n], in0=ps[7][0:64, 0:n], in1=cosw[:, p0:p0 + n], op=ALU.mult),
                 reads=[PB[7], "cosw"], writes=["t1"])
            for c in range(4):
                k.op("pe", lambda e: e.matmul(ps[7][0:64, 0:n], lhsT=wuqr[:, c, h * ROPE:(h + 1) * ROPE], rhs=qnb[:, c, 0:n], start=(c == 0), stop=(c == 3)),
                     reads=["wuqr", "qnb"], writes=[PB[7]], mark=(c == 3))
            k.op("dve", lambda e: e.tensor_tensor(out=t2[:, 0:n], in0=ps[7][0:64, 0:n], in1=sinw[:, p0:p0 + n], op=ALU.mult),
                 reads=[PB[7], "cosw"], writes=["t2"])
            k.op("dve", lambda e: e.tensor_tensor(out=qrs[si][:, 0:n], in0=t1[:, 0:n], in1=t2[:, 0:n], op=ALU.add),
                 reads=["t1", "t2"], writes=["qrs%d" % si])
            k.dma("sp", qr_scr[h, :, p0:p0 + n], qrs[si][:, 0:n], reads=["qrs%d" % si], writes=["qr_scr"], sb="qrs%d" % si)
    ph.close()

    def bank_pieces(qa, qb):
        out = []
        for b in range(3):
            lo, hi = max(qa, 512 * b), min(qb, 512 * (b + 1))
            if lo < hi:
                out.append((b, lo, hi - lo))
        return out

    prm = [(128 * i_, 128, i_, i_, (0, WINP), None) for i_ in range(NPT)]
    prm.append((NPRE, 16, NPT, None, (0, WINP), None))
    for m in range(1, 9):
        prm.append((NPRE + 16 + 128 * (m - 1), 128, NPT + m, None, (16 + 128 * (m - 1), WINP), 16 + 128 * (m - 1)))
    prob_prompt = [((0, WINP), prm)]
    prob_sample = []
    for s_ in range(NSS):
        qa = WINP + TS * s_
        tl = [(NPRE + NT + PAST * s_ + 128 * j, 128, NPT + 9 + NSS + 8 * s_ + j, None, (qa, qa + TS), None) for j in range(8)]
        tl.append((NPRE + qa, TS, NPT + 9 + s_, None, (qa, qa + TS), None))
        prob_sample.append(((qa, qa + TS), tl))

    def attention_phase(problems, kbase, ncols, vbase, nvt, qoff, nq):
        ph = Phase()
        kpeT = ph.sb("kpeT", [128, ncols], BF16)
        k.op("pool", lambda e: e.memset(kpeT[64:128, :], 0.0), writes=["kpeT"])
        k.dma("sp", kpeT[0:64, :], kpe_scr[:, kbase:kbase + ncols], writes=["kpeT"], sb="kpeT")
        kTh = [ph.sb("kTh%d" % i, [128, ncols], BF16) for i in range(2)]
        vh = [ph.sb("vh%d" % i, [128, nvt, MV], BF16) for i in range(2)]
        qnh = [ph.sb("qnh%d" % i, [128, nq], BF16) for i in range(2)]
        qrh = [ph.sb("qrh%d" % i, [128, nq], BF16) for i in range(2)]
        wgb = [ph.sb("wgb%d" % i, [128, KC, 128], BF16) for i in range(2)]
        sgb = [ph.sb("sgb%d" % i, [128, nq], F32) for i in range(2)]
        for i in range(2):
            k.op("pool", lambda e: e.memset(qrh[i][64:128, :], 0.0), writes=["qrh%d" % i])
        pTs = [ph.sb("pT%d" % i, [128, 512], BF16) for i in range(2)]
        pbias = ph.sb("pbias", [128, NPT], F32)
        k.dma("sp", pbias[:], pre_bias, writes=["pbias"], sb="pbias")
        rl = ph.sb("rl", [128, 512], F32)
        ot = ph.sb("ot", [128, 512], F32)
        st_i = [0]

        def load_head(h):
            i = h % 2
            k.dma("sp", kTh[i][:], kT_scr[h, :, kbase:kbase + ncols], writes=["kTh%d" % i], sb="kTh%d" % i)
            k.dma("sp", vh[i][:], v_scr[vbase:vbase + nvt, :, h * MV:(h + 1) * MV].rearrange("t p c -> p t c"), writes=["vh%d" % i], sb="vh%d" % i)
            k.dma("sp", qnh[i][:], qn_scr[h, :, qoff:qoff + nq], writes=["qnh%d" % i], sb="qnh%d" % i)
            k.dma("sp", qrh[i][0:64, :], qr_scr[h, :, qoff:qoff + nq], writes=["qrh%d" % i], sb="qrh%d" % i)
            wslab(wgb[i][:], w_in[:, O_GB + h * MV:O_GB + (h + 1) * MV], "wgb%d" % i)

        if MH_RUN:
            load_head(0)
        for h in range(MH_RUN):
            i = h % 2
            KT, KTN, VH, VHN = kTh[i], "kTh%d" % i, vh[i], "vh%d" % i
            QN, QNN, QR, QRN = qnh[i], "qnh%d" % i, qrh[i], "qrh%d" % i
            SG, SGN = sgb[i], "sgb%d" % i
            if h + 1 < MH_RUN:
                load_head(h + 1)
            for p0 in range(0, nq, 512):
                n = min(512, nq - p0)
                b = 6 + st_i[0] % 2
                st_i[0] += 1
                for kc in range(KC):
                    k.op("pe", lambda e: e.matmul(ps[b][:, 0:n], lhsT=wgb[i][:, kc, :], rhs=hT[:, kc, qoff + p0:qoff + p0 + n], start=(kc == 0), stop=(kc == KC - 1)),
                         reads=["wgb%d" % i, "hT"], writes=[PB[b]], mark=(kc == KC - 1))
                k.op("act", lambda e: e.activation(out=SG[:, p0:p0 + n], in_=ps[b][:, 0:n], func=AF.Sigmoid), reads=[PB[b]], writes=[SGN])
            for (qrange, tl) in problems:
                bps = bank_pieces(*qrange)
                first = {b: None for (b, _, _) in bps}
                last = {}
                for ti, (kcol, nk, vt, bi, (ra, rb), fix) in enumerate(tl):
                    for (b, lo, n) in bank_pieces(ra, rb):
                        if first[b] is None:
                            first[b] = ti
                        last[b] = ti
                items = [(ti, kcol - kbase, nk, vt - vbase, bi, fix, b, lo, n) for ti, (kcol, nk, vt, bi, (ra, rb), fix) in enumerate(tl)
                         for (b, lo, n) in bank_pieces(ra, rb)]

                def qk_exp(it, slot):
                    ti, kcol, nk, vt, bi, fix, b, lo, n = it
                    sbk = 6 + slot
                    PT, PTN = pTs[slot], "pT%d" % slot
                    k.op("pe", lambda e: e.matmul(ps[sbk][0:nk, 0:n], lhsT=KT[:, kcol:kcol + nk], rhs=QN[:, lo - qoff:lo - qoff + n], start=True, stop=False),
                         reads=[KTN, QNN], writes=[PB[sbk]], mark=False)
                    k.op("pe", lambda e: e.matmul(ps[sbk][0:nk, 0:n], lhsT=kpeT[:, kcol:kcol + nk], rhs=QR[:, lo - qoff:lo - qoff + n], start=False, stop=True),
                         reads=["kpeT", QRN], writes=[PB[sbk]])
                    if bi is not None:
                        k.op("act", lambda e: e.activation(out=PT[0:nk, 0:n], in_=ps[sbk][0:nk, 0:n], func=AF.Exp, scale=SCALE, bias=pbias[0:nk, bi:bi + 1]),
                             reads=[PB[sbk], "pbias"], writes=[PTN])
                    else:
                        k.op("act", lambda e: e.activation(out=PT[0:nk, 0:n], in_=ps[sbk][0:nk, 0:n], func=AF.Exp, scale=SCALE),
                             reads=[PB[sbk]], writes=[PTN])
                    if fix is not None and lo <= fix < lo + n:
                        f0 = fix - lo
                        k.op("pool", lambda e: e.memset(PT[64:128, f0:f0 + 64], 0.0), reads=[], writes=[PTN])

                def pv_sum(it, slot):
                    ti, kcol, nk, vt, bi, fix, b, lo, n = it
                    PT, PTN = pTs[slot], "pT%d" % slot
                    c0 = lo - 512 * b
                    k.op("pe", lambda e: e.matmul(ps[b][:, c0:c0 + n], lhsT=VH[0:nk, vt, :], rhs=PT[0:nk, 0:n],
                                                  start=(ti == first[b]), stop=(ti == last[b])), reads=[VHN, PTN], writes=[PB[b]], mark=False)
                    k.op("pe", lambda e: e.matmul(ps[3 + b][:, c0:c0 + n], lhsT=onesb[0:nk, :], rhs=PT[0:nk, 0:n],
                                                  start=(ti == first[b]), stop=(ti == last[b])), reads=["onesb", PTN], writes=[PB[3 + b]])

                for idx in range(len(items) + 1):
                    if idx < len(items):
                        qk_exp(items[idx], idx % 2)
                    if idx >= 1:
                        pv_sum(items[idx - 1], (idx - 1) % 2)
                for (b, lo, n) in bps:
                    c0 = lo - 512 * b
                    k.op("dve", lambda e: e.reciprocal(out=rl[:, 0:n], in_=ps[3 + b][:, c0:c0 + n]), reads=[PB[3 + b]], writes=["rl"])
                    k.op("dve", lambda e: e.tensor_tensor(out=ot[:, 0:n], in0=ps[b][:, c0:c0 + n], in1=rl[:, 0:n], op=ALU.mult),
                         reads=[PB[b], "rl"], writes=["ot"])
                    k.op("dve", lambda e: e.tensor_tensor(out=ot[:, 0:n], in0=ot[:, 0:n], in1=SG[:, lo - qoff:lo - qoff + n], op=ALU.mult),
                         reads=["ot", SGN], writes=["ot"])
                    k.op("dve", lambda e: e.tensor_tensor(out=mergedT[:, h, lo:lo + n], in0=mergedT[:, h, lo:lo + n], in1=ot[:, 0:n], op=ALU.add),
                         reads=["ot", "mergedT"], writes=["mergedT"])
        ph.close()

    attention_phase(prob_prompt, 0, NPRE + WINP, 0, NPT + 9, 0, WINP)
    attention_phase(prob_sample, NPRE + WINP, NSS * TS + NSS * PAST, NPT + 9, NSS + NSS * 8, WINP, NSS * TS)

    pieces = [(p0, min(512, NT - p0)) for p0 in range(0, NT, 512)]
    rstd2 = sb("rstd2", [128, NT], F32)
    ph = Phase()
    wsl = [ph.sb("wsl%d" % i, [128, KC, 128], BF16) for i in range(2)]
    xTc = [ph.sb("xTc%d" % i, [128, NT], F32) for i in range(2)]
    xnb = [ph.sb("xnb%d" % i, [128, NT], F32) for i in range(2)]
    sq2 = ph.sb("sq2", [128, NT], F32)
    gffn = ph.sb("gffn", [128, KC], F32)
    k.dma("sp", gffn[:], g_ffn_pk, writes=["gffn"], sb="gffn")
    for c in range(KC):
        W, WN = wsl[c % 2], "wsl%d" % (c % 2)
        XC, XCN = xTc[c % 2], "xTc%d" % (c % 2)
        XN_, XNN = xnb[c % 2], "xnb%d" % (c % 2)
        wslab(W[:], w_o[:, c * 128:(c + 1) * 128], WN)
        k.dma("sp", XC[:], xT_scr[c], reads=["xT_scr"], writes=[XCN], sb=XCN)
        for pi, (p0, n) in enumerate(pieces):
            for kc in range(KC):
                k.op("pe", lambda e: e.matmul(ps[pi][:, 0:n], lhsT=W[:, kc, :], rhs=mergedT[:, kc, p0:p0 + n], start=(kc == 0), stop=(kc == KC - 1)),
                     reads=[WN, "mergedT"], writes=[PB[pi]], mark=(kc == KC - 1))
            k.op("dve", lambda e: e.tensor_tensor(out=XN_[:, p0:p0 + n], in0=ps[pi][:, 0:n], in1=XC[:, p0:p0 + n], op=ALU.add),
                 reads=[PB[pi], XCN], writes=[XNN])
        k.op("act", lambda e: e.activation(out=sq2[:, :], in_=XN_[:, :], func=AF.Square), reads=[XNN], writes=["sq2"])
        for pi, (p0, n) in enumerate(pieces):
            k.op("pe", lambda e: e.matmul(ps[5 + pi][:, 0:n], lhsT=onesf[:, :], rhs=sq2[:, p0:p0 + n], start=(c == 0), stop=(c == KC - 1)),
                 reads=["onesf", "sq2"], writes=[PB[5 + pi]])
        k.op("dve", lambda e: e.tensor_scalar(out=hT[:, c, :], in0=XN_[:, :], scalar1=gffn[:, c:c + 1], scalar2=None, op0=ALU.mult),
             reads=[XNN, "gffn"], writes=["hT"])
        k.dma("sp", xn_scr[c], XN_[:, :], reads=[XNN], writes=["xn_scr"], sb=XNN)
    for pi, (p0, n) in enumerate(pieces):
        rsqrt_inplace(rstd2[:, p0:p0 + n], ps[5 + pi][:, 0:n], D, [PB[5 + pi]], ["rstd2"], epsc[:, :])
    for c in range(KC):
        k.op("dve", lambda e: e.tensor_tensor(out=hT[:, c, :], in0=hT[:, c, :], in1=rstd2[:, :], op=ALU.mult),
             reads=["hT", "rstd2"], writes=["hT"])
    ph.close()

    UE = WINP + NSS * (TS + 2)
    phg = Phase()
    gT2 = phg.sb("gT2", [128, GC - KC, NT], BF16)
    ph = Phase()

    def gTj(j):
        return (mergedT[:, j, :], "mergedT") if j < KC else (gT2[:, j - KC, :], "gT2")

    k.op("pool", lambda e: e.memset(mergedT[:], 0.0), writes=["mergedT"])
    k.op("pool", lambda e: e.memset(gT2[:], 0.0), writes=["gT2"])
    wup = [ph.sb("wup%d" % i, [128, KC, 256], BF16) for i in range(4)]
    ue = [ph.sb("ue%d" % i, [128, UE], F32) for i in range(2)]
    cc = [ph.sb("cc%d" % i, [128, UE], F32) for i in range(2)]
    histT = ph.sb("histT", [128, FC, 2 * NSS], F32)
    hrow = ph.sb("hrow", [2 * NSS, 512], F32)
    cwt = ph.sb("cwt", [128, 3, FC], F32)
    cbt = ph.sb("cbt", [128, FC], F32)
    cst = ph.sb("cst", [128, FC, 2 + 2 * NSS], F32)
    cvs = ph.sb("cvs", [2 + 2 * NSS, 512], F32)
    k.dma("sp", cwt[:], conv_w_pk, writes=["cwt"], sb="cwt")
    k.dma("sp", cbt[:], conv_b_pk, writes=["cbt"], sb="cbt")
    for g4 in range(FC // 4):
        k.dma("sp", hrow[:], c_conv[:, :, g4 * 512:(g4 + 1) * 512].rearrange("s t c -> (s t) c"), writes=["hrow"], sb="hrow")
        for j in range(4):
            k.op("pe", lambda e: e.transpose(out=ps[6][:, j * 8:(j + 1) * 8], in_=hrow[:, j * 128:(j + 1) * 128], identity=identf[0:8, 0:8]),
                 reads=["hrow", "identf"], writes=[PB[6]], mark=(j == 3))
        k.op("dve", lambda e: e.tensor_copy(out=histT[:, g4 * 4:(g4 + 1) * 4, :], in_=ps[6][:, 0:32].rearrange("p (j t) -> p j t", j=4)),
             reads=[PB[6]], writes=["histT"])
    samp = lambda ap, w, a, b: ap.rearrange("p (s t) -> p s t", t=w)[:, :, a:b]
    for j in range(GC):
        for half in range(2):
            fc = j + half * GC
            wi = (2 * (j // 2) + half) % 4
            W, WN = wup[wi], "wup%d" % wi
            U, UN = ue[half], "ue%d" % half
            C_, CN = cc[half], "cc%d" % half
            wc0 = (j % 2) * 128
            if j % 2 == 0:
                wslab(W[:], w_up[:, fc * 128:(fc + 2) * 128], WN)
            for pi, (p0, n) in enumerate(pieces):
                b = half * 3 + pi
                for kc in range(KC):
                    k.op("pe", lambda e: e.matmul(ps[b][:, 0:n], lhsT=W[:, kc, wc0:wc0 + 128], rhs=hT[:, kc, p0:p0 + n], start=(kc == 0), stop=(kc == KC - 1)),
                         reads=[WN, "hT"], writes=[PB[b]], mark=(kc == KC - 1))
                if pi < 2:
                    k.op("act", lambda e: e.copy(out=U[:, p0:p0 + n], in_=ps[b][:, 0:n]), reads=[PB[b]], writes=[UN])
                else:
                    k.op("act", lambda e: e.copy(out=U[:, 1024:WINP], in_=ps[b][:, 0:16]), reads=[PB[b]], writes=[UN])
                    k.op("act", lambda e: e.copy(out=samp(U[:, WINP:UE], TS + 2, 2, TS + 2), in_=samp(ps[b][:, 16:16 + NSS * TS], TS, 0, TS)),
                         reads=[PB[b]], writes=[UN])
            k.op("pool", lambda e: e.tensor_copy(out=samp(U[:, WINP:UE], TS + 2, 0, 2), in_=histT[:, fc, :].rearrange("p (s t) -> p s t", t=2)),
                 reads=["histT"], writes=[UN])
            k.op("pool", lambda e: e.tensor_copy(out=cst[:, fc, 0:2], in_=U[:, WINP - 2:WINP]), reads=[UN], writes=["cst"])
            k.op("pool", lambda e: e.tensor_copy(out=cst[:, fc, 2:2 + 2 * NSS].rearrange("p (s t) -> p s t", t=2), in_=samp(U[:, WINP:UE], TS + 2, TS, TS + 2)),
                 reads=[UN], writes=["cst"])
            k.op("dve", lambda e: e.tensor_scalar(out=C_[:, 0:UE - 2], in0=U[:, 2:UE], scalar1=cwt[:, 2, fc:fc + 1], scalar2=cbt[:, fc:fc + 1],
                                                  op0=ALU.mult, op1=ALU.add), reads=[UN, "cwt", "cbt"], writes=[CN])
            k.op("dve", lambda e: e.scalar_tensor_tensor(out=C_[:, 0:UE - 2], in0=U[:, 1:UE - 1], scalar=cwt[:, 1, fc:fc + 1], in1=C_[:, 0:UE - 2],
                                                         op0=ALU.mult, op1=ALU.add), reads=[UN, "cwt", CN], writes=[CN])
            k.op("dve", lambda e: e.scalar_tensor_tensor(out=C_[:, 0:UE - 2], in0=U[:, 0:UE - 2], scalar=cwt[:, 0, fc:fc + 1], in1=C_[:, 0:UE - 2],
                                                         op0=ALU.mult, op1=ALU.add), reads=[UN, "cwt", CN], writes=[CN])
        k.op("act", lambda e: e.activation(out=cc[0][:, 0:UE - 2], in_=cc[0][:, 0:UE - 2], func=AF.Silu), reads=["cc0"], writes=["cc0"])
        gt, gn = gTj(j)
        k.op("dve", lambda e: e.tensor_tensor(out=gt[:, 2:WINP], in0=cc[0][:, 0:WINP - 2], in1=cc[1][:, 0:WINP - 2], op=ALU.mult),
             reads=["cc0", "cc1"], writes=[gn])
        k.op("dve", lambda e: e.tensor_tensor(out=samp(gt[:, WINP:NT], TS, 0, TS), in0=samp(cc[0][:, WINP:UE], TS + 2, 0, TS),
                                              in1=samp(cc[1][:, WINP:UE], TS + 2, 0, TS), op=ALU.mult), reads=["cc0", "cc1"], writes=[gn])
    for g4 in range(FC // 4):
        for j in range(4):
            k.op("pe", lambda e: e.transpose(out=ps[7][0:2 + 2 * NSS, j * 128:(j + 1) * 128], in_=cst[:, g4 * 4 + j, :], identity=identf[:, :]),
                 reads=["cst", "identf"], writes=[PB[7]], mark=(j == 3))
        k.op("dve", lambda e: e.tensor_copy(out=cvs[:, :], in_=ps[7][0:2 + 2 * NSS, :]), reads=[PB[7]], writes=["cvs"])
        k.dma("sp", o_conv[:, g4 * 512:(g4 + 1) * 512], cvs[:, :], reads=["cvs"], writes=["o_conv"], sb="cvs")

    ph.close()
    ph = Phase()
    wdn = [ph.sb("wdn%d" % i, [128, GC, 256], BF16) for i in range(2)]
    xnc = [ph.sb("xnc%d" % i, [128, NT], F32) for i in range(2)]
    xos = [ph.sb("xos%d" % i, [128, NT], F32) for i in range(2)]
    for c in range(KC):
        W, WN = wdn[(c // 2) % 2], "wdn%d" % ((c // 2) % 2)
        XC, XCN = xnc[c % 2], "xnc%d" % (c % 2)
        XO, XON = xos[c % 2], "xos%d" % (c % 2)
        wd0 = (c % 2) * 128
        if c % 2 == 0:
            wslab(W[:], w_down[:, c * 128:(c + 2) * 128], WN)
        k.dma("sp", XC[:], xn_scr[c], reads=["xn_scr"], writes=[XCN], sb=XCN)
        for pi, (p0, n) in enumerate(pieces):
            for j in range(GC):
                gt, gn = gTj(j)
                k.op("pe", lambda e: e.matmul(ps[pi][:, 0:n], lhsT=W[:, j, wd0:wd0 + 128], rhs=gt[:, p0:p0 + n], start=(j == 0), stop=(j == GC - 1)),
                     reads=[WN, gn], writes=[PB[pi]], mark=(j == GC - 1))
            k.op("dve", lambda e: e.tensor_tensor(out=XO[:, p0:p0 + n], in0=ps[pi][:, 0:n], in1=XC[:, p0:p0 + n], op=ALU.add),
                 reads=[PB[pi], XCN], writes=[XON])
        k.dma("sp", xo_scr[c], XO[:, :], reads=[XON], writes=["xo_scr"], sb=XON)
    ph.close()
    phg.close()

    ph = Phase()
    finb = ph.sb("finb", [128, D], F32)
    k.dma("sp", finb[:], fin_bc, writes=["finb"], sb="finb")
    xo_t = [ph.sb("xo_t%d" % i, [128, KC, 128], F32) for i in range(2)]
    ysb = [ph.sb("ysb%d" % i, [128, D], F32) for i in range(2)]
    junk = ph.sb("junk", [128, 512], F32)
    ssy = ph.sb("ssy", [128, 8], F32)
    for ti, (t0, n) in enumerate([(t0, min(128, NT - t0)) for t0 in range(0, NT, 128)]):
        XT, XTN = xo_t[ti % 2], "xo_t%d" % (ti % 2)
        Y, YN = ysb[ti % 2], "ysb%d" % (ti % 2)
        k.dma("sp", XT[:, :, 0:n], xo_scr[:, :, t0:t0 + n].rearrange("kc p t -> p kc t"), reads=["xo_scr"], writes=[XTN], sb=XTN)
        for g in range(4):
            b = (ti % 2) * 4 + g
            for j in range(4):
                kc = g * 4 + j
                k.op("pe", lambda e: e.transpose(out=ps[b][0:n, j * 128:(j + 1) * 128], in_=XT[:, kc, 0:n], identity=identf[:, :]),
                     reads=[XTN, "identf"], writes=[PB[b]], mark=(j == 3))
            k.op("act", lambda e: e.activation(out=junk[0:n, :], in_=ps[b][0:n, :], func=AF.Square, accum_out=ssy[0:n, g:g + 1]),
                 reads=[PB[b]], writes=["junk", "ssy"])
        k.op("dve", lambda e: e.tensor_tensor(out=ssy[0:n, 4:5], in0=ssy[0:n, 0:1], in1=ssy[0:n, 1:2], op=ALU.add), reads=["ssy"], writes=["ssy"])
        k.op("dve", lambda e: e.tensor_tensor(out=ssy[0:n, 5:6], in0=ssy[0:n, 2:3], in1=ssy[0:n, 3:4], op=ALU.add), reads=["ssy"], writes=["ssy"])
        k.op("dve", lambda e: e.tensor_tensor(out=ssy[0:n, 6:7], in0=ssy[0:n, 4:5], in1=ssy[0:n, 5:6], op=ALU.add), reads=["ssy"], writes=["ssy"])
        rsqrt_inplace(ssy[0:n, 7:8], ssy[0:n, 6:7], D, ["ssy"], ["ssy"], epsc[0:n, :])
        for g in range(4):
            b = (ti % 2) * 4 + g
            k.op("dve", lambda e: e.scalar_tensor_tensor(out=Y[0:n, g * 512:(g + 1) * 512], in0=ps[b][0:n, :], scalar=ssy[0:n, 7:8],
                                                         in1=finb[0:n, g * 512:(g + 1) * 512], op0=ALU.mult, op1=ALU.mult),
                 reads=[PB[b], "ssy", "finb"], writes=[YN])
        k.dma("sp", o_y[t0:t0 + n, :], Y[0:n, :], reads=[YN], writes=["o_y"], sb=YN)
    ph.close()

    if K_DBG:
        o_dbg = dout("o_dbg", [128, KC, NT], BF16)
        k.dma("sp", o_dbg, mergedT[:], reads=["mergedT"], writes=["o_dbg"], sb="mergedT")
    k.barrier()
    return nc


_CACHE = {}


def _rope_tables(pos):
    inv = (10000.0 ** (-np.arange(0, ROPE, 2, dtype=np.float32) / ROPE)).astype(np.float32)
    ang = pos.astype(np.float32)[:, None] * inv[None, :]
    c, s = np.cos(ang).astype(np.float32), np.sin(ang).astype(np.float32)
    return (np.ascontiguousarray(np.concatenate([c, c], 1).T),
            np.ascontiguousarray(np.concatenate([-s, s], 1).T))


def kernel(x_prompt, x_sample, cache_mla_latent, cache_mla_krope, state_gla, cache_ffn_conv, meta_tokens,
           g_mix, w_in, w_a2, b_a, g_gla_out, g_q, w_uq, g_kv, w_uk, w_uv, w_o, g_ffn, w_up, conv_w, conv_b,
           w_down, final_norm):
    f = lambda a: np.ascontiguousarray(np.asarray(a, dtype=np.float32))
    x_prompt, x_sample, meta_tokens, w_in = f(x_prompt), f(x_sample), f(meta_tokens), f(w_in)
    if "nc" not in _CACHE:
        _CACHE["nc"] = build_program()
    nc = _CACHE["nc"]
    ext = np.concatenate([meta_tokens, x_prompt[0]], 0)
    pk = lambda v, n: np.ascontiguousarray(f(v).reshape(n, 128).T)
    bc = lambda v: np.ascontiguousarray(np.broadcast_to(f(v).reshape(1, -1), (128, f(v).size)))
    perm = np.concatenate([np.arange(32, 64), np.arange(0, 32)])
    cos_p, sin_p = _rope_tables(np.arange(NPRE))
    shared = {
        "x_pre": np.ascontiguousarray(ext[:NPRE]),
        "cos_pre": cos_p, "sin_pre": sin_p,
        "w_in": w_in[0], "w_kpe_rot": np.ascontiguousarray(w_in[0][:, O_KPE + perm]),
        "w_a2": f(w_a2)[0], "b_a_bc": bc(f(b_a)[0]), "g_mix_bc": bc(f(g_mix)[0]),
        "g_gla_pk": pk(f(g_gla_out)[0], 4), "g_q_pk": pk(f(g_q)[0], 4), "g_kv_pk": pk(f(g_kv)[0], 4),
        "g_ffn_pk": pk(f(g_ffn)[0], KC), "fin_bc": bc(final_norm),
        "w_uq": f(w_uq)[0],
        "w_uq_rot": np.ascontiguousarray(f(w_uq)[0].reshape(QR, MH, NOPE + ROPE)[:, :, NOPE + perm].reshape(QR, MH * ROPE)),
        "w_uk": f(w_uk)[0], "w_uv": f(w_uv)[0], "w_o": f(w_o)[0], "w_up": f(w_up)[0],
        "conv_w_pk": np.ascontiguousarray(f(conv_w)[0].reshape(3, FC, 128).transpose(2, 0, 1)),
        "conv_b_pk": pk(f(conv_b)[0], FC), "w_down": f(w_down)[0],
        "c_ident": np.eye(128, dtype=np.float32), "c_ls": np.tril(np.ones((128, 128), np.float32), -1), "c_ui": np.triu(np.ones((128, 128), np.float32)), "c_ones": np.ones((128, 128), np.float32),
    }
    in_maps = []
    for c in range(NCORES):
        pos = np.concatenate([np.arange(1024 * c, 1024 * c + WINP)] + [PAST + np.arange(TS)] * NSS)
        cw, sw = _rope_tables(pos)
        m = dict(shared)
        m["x_win"] = np.ascontiguousarray(np.concatenate(
            [ext[1024 * c:1024 * c + WINP], x_sample[NSS * c:NSS * c + NSS].reshape(NSS * TS, D)], 0))
        valid = (np.arange(NPT)[None, :] < 8 * c).astype(np.float32) * np.ones((128, 1), np.float32)
        m["pre_valid"] = np.ascontiguousarray(valid)
        m["pre_bias"] = np.ascontiguousarray((1.0 - valid) * NEG)
        m["cos_win"], m["sin_win"] = cw, sw
        m["c_lat"] = f(cache_mla_latent)[0, NSS * c:NSS * c + NSS]
        m["c_kr"] = f(cache_mla_krope)[0, NSS * c:NSS * c + NSS]
        m["c_gla"] = f(state_gla)[0, NSS * c:NSS * c + NSS]
        m["c_conv"] = f(cache_ffn_conv)[0, NSS * c:NSS * c + NSS]
        in_maps.append(m)
    used = _CACHE.get("used")
    if used is None:
        used = _CACHE["used"] = set(_input_names(nc))
    in_maps = [{kk: vv for kk, vv in m.items() if kk in used} for m in in_maps]
    res = run_bass_kernel_spmd(nc, in_maps, core_ids=list(range(NCORES))).results

    y_p = np.zeros((1, SEQ, D), np.float32)
    y_s = np.zeros((32, TS, D), np.float32)
    lat_p = np.zeros((1, 1, EXT, KVR), np.float32)
    kr_p = np.zeros((1, 1, EXT, ROPE), np.float32)
    gla_p = np.zeros((1, 1, GH, GDK, GDV), np.float32)
    conv_p = np.zeros((1, 1, 2, 2 * DFF), np.float32)
    lat_s = np.zeros((1, 32, TS, KVR), np.float32)
    kr_s = np.zeros((1, 32, TS, ROPE), np.float32)
    gla_s = np.zeros((1, 32, GH, GDK, GDV), np.float32)
    conv_s = np.zeros((1, 32, 2, 2 * DFF), np.float32)
    for c in range(NCORES):
        r = res[c]
        lo = 0 if c == 0 else 16
        lat_p[0, 0, 1024 * c + lo:1024 * c + WINP] = r["o_lat"][lo:WINP]
        kr_p[0, 0, 1024 * c + lo:1024 * c + WINP] = r["o_kr"][lo:WINP]
        lat_s[0, NSS * c:NSS * c + NSS] = r["o_lat"][WINP:].reshape(NSS, TS, KVR)
        kr_s[0, NSS * c:NSS * c + NSS] = r["o_kr"][WINP:].reshape(NSS, TS, ROPE)
        if c == NCORES - 1:
            gla_p[0, 0] = r["o_gla_p"]
        gla_s[0, NSS * c:NSS * c + NSS] = r["o_gla_s"]
        if c == NCORES - 1:
            conv_p[0, 0] = r["o_conv"][0:2]
        conv_s[0, NSS * c:NSS * c + NSS] = r["o_conv"][2:].reshape(NSS, 2, 2 * DFF)
        if "o_dbg" in r and c == 0:
            _CACHE["dbg"] = np.asarray(r["o_dbg"]).astype(np.float32)
        if "o_y" in r:
            y_p[0, 1024 * c:1024 * c + 1024] = r["o_y"][16:WINP]
            y_s[NSS * c:NSS * c + NSS] = r["o_y"][WINP:].reshape(NSS, TS, D)
    return (y_p, y_s, lat_p, kr_p, gla_p, conv_p, lat_s, kr_s, gla_s, conv_s)


def _input_names(nc):
    return list(IN_NAMES)
```
